# Optimizing a Trainium2 kernel written in Bass

```python
import math
import jax, jax.numpy as jnp
from jax import lax
import numpy as np

D_MODEL = 1024
BATCH = 2
SEQ = 8192
DEPTH = 2
DEC_BATCH = 8
DEC_SEQ = 2048
PAST_LEN = 128

N_MEM = 256
HEAD_DIM = 64
ATTN_GROUPS = ((128, 1), (512, 4), (2048, 16))
HEADS_PER_GROUP = 4
N_ATTN_HEADS = HEADS_PER_GROUP * len(ATTN_GROUPS)
D_ATTN = N_ATTN_HEADS * HEAD_DIM
D_ATTN_OUT = HEADS_PER_GROUP * HEAD_DIM
QB = 64
NEG = -1e30
LRU_BLOCKS = 12
LRU_BLOCK = 64
D_LRU = LRU_BLOCKS * LRU_BLOCK
CONV_W = 4
LRU_C = 8.0
N_XHEADS = 4
XHEAD_DIM = 128
D_X = N_XHEADS * XHEAD_DIM
N_BRANCH = 3
D_FF = 2816
N_BUCKETS = 32
MAX_DIST = 1024
ALPHA = (2 * DEPTH) ** 0.25
BETA = (8 * DEPTH) ** -0.25
LN_EPS = 1e-5
D_IN = 3 * D_ATTN + 2 * D_LRU + D_X + N_BRANCH * D_MODEL
SPLITS = [D_ATTN, 2 * D_ATTN, 3 * D_ATTN, 3 * D_ATTN + D_LRU, 3 * D_ATTN + 2 * D_LRU,
          3 * D_ATTN + 2 * D_LRU + D_X]

kernel_name = "hybrid_dilated_rglru_memory_encoder"


def layer_norm(x, g, b):
    xf = x.astype(jnp.float32)
    mu = jnp.mean(xf, axis=-1, keepdims=True)
    var = jnp.mean(jnp.square(xf - mu), axis=-1, keepdims=True)
    return ((xf - mu) * lax.rsqrt(var + LN_EPS) * g + b).astype(x.dtype)


def swiglu(x, w_in, w_out):
    gt, up = jnp.split(x @ w_in, 2, axis=-1)
    return (jax.nn.silu(gt) * up) @ w_out


def t5_bucket(rel):
    half = N_BUCKETS // 2
    max_exact = half // 2
    sign = (rel > 0).astype(np.int32) * half
    n = np.abs(rel)
    large = max_exact + (np.log(np.maximum(n, 1) / max_exact) / math.log(MAX_DIST / max_exact)
                         * (half - max_exact)).astype(np.int32)
    large = np.minimum(large, half - 1)
    return sign + np.where(n < max_exact, n, large)


def dilated_group_attention(q, k, v, bias_g, dilation, window):
    B, S, H, Dh = q.shape
    half = window // (2 * dilation)
    n = S // dilation
    nb = -(-n // QB)
    n_pad = nb * QB

    def to_blocks(t):
        t = t.reshape(B, n, dilation, H, Dh).transpose(0, 2, 1, 3, 4)
        t = jnp.pad(t, ((0, 0), (0, 0), (0, n_pad - n), (0, 0), (0, 0)))
        return t.reshape(B, dilation, nb, QB, H, Dh)

    def windows(t):
        tb = jnp.pad(to_blocks(t), ((0, 0), (0, 0), (1, 1), (0, 0), (0, 0), (0, 0)))
        return jnp.concatenate([tb[:, :, :-2], tb[:, :, 1:-1], tb[:, :, 2:]], axis=3)

    qb = to_blocks(q)
    kw = windows(k)
    vw = windows(v).astype(jnp.float32)

    t_idx = np.arange(QB)[:, None]
    u_idx = np.arange(3 * QB)[None, :]
    delta = u_idx - QB - t_idx
    kpos = np.arange(nb)[:, None, None] * QB + u_idx[None] - QB
    valid = (np.abs(delta) <= half)[None] & (kpos >= 0) & (kpos < n)
    bias = jnp.transpose(bias_g[t5_bucket(delta * dilation)], (2, 0, 1)).astype(jnp.float32)

    s = jnp.einsum('bgnqhe,bgnkhe->bgnhqk', qb, kw).astype(jnp.float32) * (HEAD_DIM ** -0.5) + bias
    s = jnp.where(valid[None, None, :, None], s, NEG)
    m = jnp.max(s, axis=-1, keepdims=True)
    p = jnp.exp(s - m)
    l = jnp.sum(p, axis=-1, keepdims=True)
    o = jnp.einsum('bgnhqk,bgnkhe->bgnqhe', p, vw) / jnp.swapaxes(l, 3, 4)
    lse = jnp.swapaxes((m + jnp.log(l))[..., 0], 3, 4)

    o = o.reshape(B, dilation, n_pad, H, Dh)[:, :, :n].transpose(0, 2, 1, 3, 4).reshape(B, S, H, Dh)
    lse = lse.reshape(B, dilation, n_pad, H)[:, :, :n].transpose(0, 2, 1, 3).reshape(B, S, H)
    return o, lse


def centred_dwconv(x, w, b):
    S = x.shape[1]
    lo = (CONV_W - 1) // 2
    xp = jnp.pad(x, ((0, 0), (lo, CONV_W - 1 - lo), (0, 0)))
    return sum(xp[:, j:j + S] * w[j] for j in range(CONV_W)) + b


def rg_lru_scan(x, w_a, b_a, w_x, b_x, lam, reverse):
    B, S, _ = x.shape
    xb = x.reshape(B, S, LRU_BLOCKS, LRU_BLOCK)
    r = jax.nn.sigmoid(jnp.einsum('bshi,hij->bshj', xb, w_a).reshape(B, S, D_LRU) + b_a)
    i = jax.nn.sigmoid(jnp.einsum('bshi,hij->bshj', xb, w_x).reshape(B, S, D_LRU) + b_x)
    log_a = -LRU_C * r.astype(jnp.float32) * jax.nn.softplus(-lam.astype(jnp.float32))
    a = jnp.exp(log_a)
    u = jnp.sqrt(-jnp.expm1(2.0 * log_a)) * (i * x).astype(jnp.float32)

    def combine(c1, c2):
        a1, b1 = c1
        a2, b2 = c2
        return a1 * a2, a2 * b1 + b2

    _, h = lax.associative_scan(combine, (a, u), reverse=reverse, axis=1)
    return h


def memory_attention(q, mem, w_mem_kv):
    B, S, _ = q.shape
    k, v = jnp.split(mem @ w_mem_kv, 2, axis=-1)
    q = q.reshape(B, S, N_XHEADS, XHEAD_DIM)
    k = k.reshape(B, -1, N_XHEADS, XHEAD_DIM)
    v = v.reshape(B, -1, N_XHEADS, XHEAD_DIM)
    s = jnp.einsum('bshe,bmhe->bhsm', q, k).astype(jnp.float32) * (XHEAD_DIM ** -0.5)
    p = jax.nn.softmax(s, axis=-1)
    o = jnp.einsum('bhsm,bmhe->bshe', p, v.astype(jnp.float32))
    return o.reshape(B, S, D_X).astype(q.dtype)


def encoder_layer(x, mem, rel_bias, w_in, b_gate, conv_w, conv_b, lru_wa, lru_ba, lru_wx, lru_bx,
                  lru_lambda, w_mem_kv, w_br_attn, w_br_lru, w_br_mem, w_out, ff_in, ff_out, ln_g, ln_b):
    B, S, _ = x.shape
    x = layer_norm(ALPHA * x + 0.5 * swiglu(x, ff_in[0], ff_out[0]), ln_g[0], ln_b[0])

    q, k, v, xr, gr, qm, gates = jnp.split(x @ w_in, SPLITS, axis=-1)

    q = q.reshape(B, S, N_ATTN_HEADS, HEAD_DIM)
    k = k.reshape(B, S, N_ATTN_HEADS, HEAD_DIM)
    v = v.reshape(B, S, N_ATTN_HEADS, HEAD_DIM)
    outs, lses = [], []
    for gi, (win, dil) in enumerate(ATTN_GROUPS):
        hs = slice(gi * HEADS_PER_GROUP, (gi + 1) * HEADS_PER_GROUP)
        o, lse = dilated_group_attention(q[:, :, hs], k[:, :, hs], v[:, :, hs], rel_bias[:, hs], dil, win)
        outs.append(o)
        lses.append(lse)
    wgt = jax.nn.softmax(jnp.stack(lses, axis=0), axis=0)
    attn = jnp.einsum('gbsh,gbshe->bshe', wgt, jnp.stack(outs, axis=0))
    attn = attn.reshape(B, S, D_ATTN_OUT).astype(x.dtype)

    xc = centred_dwconv(xr, conv_w, conv_b)
    h_fwd = rg_lru_scan(xc, lru_wa[0], lru_ba[0], lru_wx[0], lru_bx[0], lru_lambda[0], reverse=False)
    h_bwd = rg_lru_scan(xc, lru_wa[1], lru_ba[1], lru_wx[1], lru_bx[1], lru_lambda[1], reverse=True)
    rec = jax.nn.gelu(gr) * (h_fwd + h_bwd).astype(x.dtype)

    xm = memory_attention(qm, mem, w_mem_kv)

    g = jax.nn.sigmoid(gates.reshape(B, S, N_BRANCH, D_MODEL) + b_gate)
    merged = (g[:, :, 0] * (attn @ w_br_attn) + g[:, :, 1] * (rec @ w_br_lru)
              + g[:, :, 2] * (xm @ w_br_mem))
    x = layer_norm(ALPHA * x + merged @ w_out, ln_g[1], ln_b[1])

    x = layer_norm(ALPHA * x + 0.5 * swiglu(x, ff_in[1], ff_out[1]), ln_g[2], ln_b[2])
    return x


def encoder_trunk(x, mem, rel_bias, layer_weights):
    for layer in range(DEPTH):
        x = encoder_layer(x, mem, rel_bias, *[w[layer] for w in layer_weights])
    return x


def setup_inputs(seed: int = 0) -> dict:
    key = jax.random.key(seed)
    ks = jax.random.split(key, 24)
    f32 = jnp.float32

    def nrm(k, shape, scale):
        return jax.random.normal(k, shape, f32) * scale

    v_cols = jnp.zeros((D_IN,), bool).at[2 * D_ATTN:3 * D_ATTN].set(True)
    w_in = nrm(ks[5], (DEPTH, D_MODEL, D_IN), D_MODEL ** -0.5) * jnp.where(v_cols, BETA, 1.0)
    kv_scale = jnp.concatenate([jnp.ones((D_X,), f32), jnp.full((D_X,), BETA, f32)])
    a = jax.random.uniform(ks[13], (DEPTH, 2, D_LRU), f32, 0.9, 0.999) ** (1.0 / LRU_C)
    return {
        'x_prompt': nrm(ks[0], (BATCH, SEQ, D_MODEL), 1.0),
        'x_sample': nrm(ks[1], (DEC_BATCH, DEC_SEQ, D_MODEL), 1.0),
        'mem_prompt': nrm(ks[2], (BATCH, N_MEM, D_MODEL), 1.0),
        'mem_sample': nrm(ks[3], (DEC_BATCH, N_MEM, D_MODEL), 1.0),
        'rel_bias': nrm(ks[4], (N_BUCKETS, N_ATTN_HEADS), 0.5),
        'w_in': w_in,
        'b_gate': nrm(ks[6], (DEPTH, N_BRANCH, D_MODEL), 0.1),
        'conv_w': nrm(ks[7], (DEPTH, CONV_W, D_LRU), CONV_W ** -0.5),
        'conv_b': nrm(ks[8], (DEPTH, D_LRU), 0.02),
        'lru_wa': nrm(ks[9], (DEPTH, 2, LRU_BLOCKS, LRU_BLOCK, LRU_BLOCK), LRU_BLOCK ** -0.5),
        'lru_ba': nrm(ks[10], (DEPTH, 2, D_LRU), 0.1),
        'lru_wx': nrm(ks[11], (DEPTH, 2, LRU_BLOCKS, LRU_BLOCK, LRU_BLOCK), LRU_BLOCK ** -0.5),
        'lru_bx': nrm(ks[12], (DEPTH, 2, D_LRU), 0.1),
        'lru_lambda': jnp.log(a) - jnp.log1p(-a),
        'w_mem_kv': nrm(ks[14], (DEPTH, D_MODEL, 2 * D_X), D_MODEL ** -0.5) * kv_scale,
        'w_br_attn': nrm(ks[15], (DEPTH, D_ATTN_OUT, D_MODEL), BETA * D_ATTN_OUT ** -0.5),
        'w_br_lru': nrm(ks[16], (DEPTH, D_LRU, D_MODEL), BETA * D_LRU ** -0.5),
        'w_br_mem': nrm(ks[17], (DEPTH, D_X, D_MODEL), BETA * D_X ** -0.5),
        'w_out': nrm(ks[18], (DEPTH, D_MODEL, D_MODEL), BETA * D_MODEL ** -0.5),
        'ff_in': nrm(ks[19], (DEPTH, 2, D_MODEL, 2 * D_FF), D_MODEL ** -0.5),
        'ff_out': nrm(ks[20], (DEPTH, 2, D_FF, D_MODEL), BETA * D_FF ** -0.5),
        'ln_g': 1.0 + nrm(ks[21], (DEPTH, 3, D_MODEL), 0.02),
        'ln_b': nrm(ks[22], (DEPTH, 3, D_MODEL), 0.02),
    }


def reference(x_prompt, x_sample, mem_prompt, mem_sample, rel_bias, w_in, b_gate, conv_w, conv_b,
              lru_wa, lru_ba, lru_wx, lru_bx, lru_lambda, w_mem_kv, w_br_attn, w_br_lru, w_br_mem,
              w_out, ff_in, ff_out, ln_g, ln_b):
    layer_weights = (w_in, b_gate, conv_w, conv_b, lru_wa, lru_ba, lru_wx, lru_bx, lru_lambda,
                     w_mem_kv, w_br_attn, w_br_lru, w_br_mem, w_out, ff_in, ff_out, ln_g, ln_b)
    y_prompt = encoder_trunk(x_prompt, mem_prompt, rel_bias, layer_weights)
    y_sample = encoder_trunk(x_sample, mem_sample, rel_bias, layer_weights)
    return (y_prompt, y_sample)
```

```python
import contextlib
import math
import numpy as np
import ml_dtypes
import concourse.bass as bass
import concourse.mybir as mybir
from concourse.bass_utils import run_bass_kernel_spmd

F32 = mybir.dt.float32
BF16 = mybir.dt.bfloat16
AF = mybir.ActivationFunctionType
ALU = mybir.AluOpType
ENGS = ("tensor", "vector", "scalar", "gpsimd", "sync")

D = 1024
DFF = 2816
NFF = 22
SEG = 2048
TT = 512
TPS = SEG // TT
ALPHA = 4.0 ** 0.25
C0 = 0.5 / ALPHA
EPS2 = 1e-5 / (ALPHA * ALPHA)
DILS = (1, 4, 16)
GELU_C = math.sqrt(2.0 / math.pi)
NPIECE = 61
OQ, OK_, OV, OXR, OGR, OQM, OGT = 0, 768, 1536, 2304, 3072, 3840, 4352


class T:
    def __init__(self, prog, h, name):
        self.p = prog
        self.h = h
        self.name = name
        self.w = []
        self.r = []
        self.ent = None

    def __getitem__(self, idx):
        return self.h[idx]


class SemEnt:
    def __init__(self, sem, i):
        self.sem = sem
        self.cnt = 0
        self.id = i


class Scope:
    def __init__(self, prog):
        self.p = prog
        self.es = contextlib.ExitStack()
        self.ts = []

    def sbuf(self, name, shape, dt):
        self.p.uid += 1
        name = "%s_u%d" % (name, self.p.uid)
        h = self.es.enter_context(self.p.nc.sbuf_tensor(name, list(shape), dt))
        t = T(self.p, h, name)
        self.ts.append(t)
        return t

    def close(self):
        self.p.barrier()
        for t in self.ts:
            if t.ent is not None:
                self.p.freeents.append(t.ent)
                t.ent = None
        self.es.close()


class Prog:
    def __init__(self, nc):
        self.nc = nc
        self.es = contextlib.ExitStack()
        self.q = {e: [] for e in ENGS}
        self.tick = {e: 0 for e in ENGS}
        self.sem = {e: self.es.enter_context(nc.semaphore("s_" + e)) for e in ENGS}
        self.known = {e: {} for e in ENGS}
        self.ents = []
        self.freeents = []
        self.ninst = 0
        self.uid = 0

    def scope(self):
        return Scope(self)

    def sbuf(self, name, shape, dt):
        self.uid += 1
        name = "%s_u%d" % (name, self.uid)
        h = self.es.enter_context(self.nc.sbuf_tensor(name, list(shape), dt))
        return T(self, h, name)

    def psum(self, name, shape, dt):
        h = self.es.enter_context(self.nc.psum_tensor(name, list(shape), dt))
        return T(self, h, name)

    def dram(self, name, shape, dt, kind=None):
        if kind is None:
            h = self.nc.dram_tensor(name, list(shape), dt)
        else:
            h = self.nc.dram_tensor(name, list(shape), dt, kind=kind)
        return T(self, h, name)

    def token(self, name):
        return T(self, None, name)

    def _ent(self, t):
        if t.ent is None:
            if self.freeents:
                t.ent = self.freeents.pop()
            else:
                s = self.es.enter_context(self.nc.semaphore("d%d" % len(self.ents)))
                t.ent = SemEnt(s, len(self.ents))
                self.ents.append(t.ent)
        return t.ent

    def _wait(self, eng, deps):
        kn = self.known[eng]
        best = {}
        for d in deps:
            if d[0] == 'e':
                if d[1] == eng and eng == 'tensor':
                    continue
                key = ('e', d[1])
                val = d[2]
                sem = self.sem[d[1]]
            else:
                key = ('d', d[3])
                val = d[2]
                sem = d[1]
            if kn.get(key, 0) >= val:
                continue
            if key not in best or best[key][1] < val:
                best[key] = (sem, val)
        for key, (sem, val) in best.items():
            kn[key] = val
            self.q[eng].append(lambda e, sem=sem, val=val: e.wait_ge(sem, val))

    def I(self, eng, name, *args, reads=(), writes=(), **kw):
        deps = []
        for b in reads:
            deps += b.w
        for b in writes:
            deps += b.w
            deps += b.r
        self._wait(eng, deps)
        self.tick[eng] += 1
        tk = self.tick[eng]
        sem = self.sem[eng]
        self.q[eng].append(lambda e, name=name, args=args, kw=kw, sem=sem: getattr(e, name)(*args, **kw).then_inc(sem, 1))
        me = ('e', eng, tk)
        for b in writes:
            b.w = [me]
            b.r = []
        for b in reads:
            if b not in writes:
                b.r.append(me)
        self.ninst += 1

    def dma(self, eng, out_ap, in_ap, reads=(), writes=(), semfrom=None):
        deps = []
        for b in reads:
            deps += b.w
        for b in writes:
            deps += b.w
            deps += b.r
        self._wait(eng, deps)
        ent = self._ent(semfrom)
        ent.cnt += 16
        sem = ent.sem
        self.q[eng].append(lambda e, o=out_ap, i=in_ap, sem=sem:
                           e.dma_start(out=o, in_=i).then_inc(sem, 16))
        me = ('d', sem, ent.cnt, ent.id)
        for b in writes:
            b.w = [me]
            b.r = []
        for b in reads:
            if b not in writes:
                b.r.append(me)
        self.ninst += 1

    def barrier(self):
        for e in ENGS:
            deps = [('e', o, self.tick[o]) for o in ENGS if self.tick[o] > 0 and not (o == e and e == 'tensor')]
            deps += [('d', en.sem, en.cnt, en.id) for en in self.ents if en.cnt > 0]
            self._wait(e, deps)

    def emit(self):
        nc = self.nc
        with nc.allow_non_contiguous_dma(reason="halo columns / strided scratch layouts"), nc.Block() as block:
            for e in ENGS:
                lst = self.q[e]
                if not lst:
                    continue

                def body(engine, lst=lst):
                    for f in lst:
                        f(engine)
                getattr(block, e)(body)
        self.es.close()


class Rot:
    def __init__(self, items):
        self.items = items
        self.i = 0

    def next(self):
        t = self.items[self.i % len(self.items)]
        self.i += 1
        return t


def piece_index():
    names = []
    for f in range(2):
        names += [("ffi", f, j) for j in range(11)]
        names += [("ffo", f, j) for j in range(6)]
    names += [("q", 0), ("q", 1), ("k", 0), ("k", 1), ("v", 0), ("v", 1), ("v", 2),
              ("xr", 0), ("xr", 1), ("gr", 0), ("gr", 1), ("qm", 0)]
    names += [("g", j) for j in range(6)]
    names += [("mkv", 0), ("mkv", 1), ("bra", 0), ("brl", 0), ("brl", 1), ("brm", 0), ("wo", 0), ("wo", 1),
              ("lru", 0)]
    assert len(names) == NPIECE
    return {n: i for i, n in enumerate(names)}, names


PIDX, PNAMES = piece_index()


def build_program(NSEG, dbg=False):
    NTOK = NSEG * SEG
    NT = NTOK // TT
    nc = bass.Bass("TRN2", target_bir_lowering=False)
    P = Prog(nc)

    def din(name, shape, dt=F32):
        return P.dram(name, shape, dt, kind="ExternalInput")

    x_in = din("x_in", [NTOK, D])
    mem_in = din("mem_in", [NSEG, 256, D])
    flags_in = din("flags", [128, NSEG * 4])
    w_in = din("w_in", [2, D, 7424])
    ff_in = din("ff_in", [2, 2, D, 2 * DFF])
    ff_out = din("ff_out", [2, 2, DFF, D])
    w_mem_kv = din("w_mem_kv", [2, D, 1024])
    w_br_attn = din("w_br_attn", [2, 256, D])
    w_br_lru = din("w_br_lru", [2, 768, D])
    w_br_mem = din("w_br_mem", [2, 512, D])
    w_out = din("w_out", [2, D, D])
    lru_wa = din("lru_wa", [2, 2, 12, 64, 64])
    lru_wx = din("lru_wx", [2, 2, 12, 64, 64])
    bgate_c = din("bgate_c", [2, 128, 24])
    convw_c = din("convw_c", [2, 128, 24])
    convb_c = din("convb_c", [2, 128, 6])
    lba_c = din("lba_c", [2, 128, 12])
    lbx_c = din("lbx_c", [2, 128, 12])
    llam_c = din("llam_c", [2, 128, 12])
    lng_r = din("lng_r", [2, 3, 128, D])
    lnb_r = din("lnb_r", [2, 3, 128, D])
    biasg = din("biasg", [12, 128, 256])
    maskc = din("maskc", [128, 256])
    ident_in = din("ident", [128, 128], BF16)
    y_out = P.dram("y", [NTOK, D], F32, kind="ExternalOutput")

    wb = P.dram("wb", [2 * NPIECE, 128, 4096], BF16)
    x1s = P.dram("x1s", [NTOK, D], F32)
    xl = P.dram("xl", [NTOK, D], F32)
    x1T = P.dram("x1T", [D, NTOK], BF16)
    qT = P.dram("qT", [768, NTOK], BF16)
    kT = P.dram("kT", [768, NSEG, 4096], BF16)
    vG = [P.dram("vG%d" % g, [NSEG, DILS[g], SEG // DILS[g] + 128, 256], BF16) for g in range(3)]
    XRW = SEG + 4
    xrT = P.dram("xrT", [768, NSEG, XRW], BF16)
    grT = P.dram("grT", [768, NTOK], BF16)
    qmT = P.dram("qmT", [512, NTOK], BF16)
    hfT = P.dram("hfT", [768, NTOK], F32)
    xcT = P.dram("xcT", [768, NTOK], F32)
    hsT = P.dram("hsT", [768, NTOK], BF16)
    attnT = P.dram("attnT", [256, NTOK], BF16)
    xmT = P.dram("xmT", [512, NTOK], BF16)
    dbg_out = {}

    ident = P.sbuf("ident", [128, 128], BF16)
    ones_bf = P.sbuf("ones_bf", [128, 128], BF16)
    flags = P.sbuf("flags_sb", [128, NSEG * 4], F32)
    pbanks = [P.psum("pb%d" % i, [128, 512], F32) for i in range(8)]
    prot = Rot(pbanks)

    P.dma("sync", ident[:, :], ident_in[:, :], writes=[ident], semfrom=ident)
    P.dma("sync", flags[:, :], flags_in[:, :], writes=[flags], semfrom=flags)
    P.I("vector", "memset", ones_bf[:, :], 1.0, writes=[ones_bf])

    def piece_srcs(layer, name):
        kind = name[0]
        out = []

        def kc_view(W2, k0, nk, c0, ncol):
            return W2[k0:k0 + nk * 128, c0:c0 + ncol].rearrange("(kc p) n -> p kc n", p=128)

        if kind == "ffi":
            f, j = name[1], name[2]
            W2 = ff_in[layer, f]
            out.append((lambda st: st[:, :].rearrange("p (kc n) -> p kc n", kc=8)[:, :, 0:256],
                        kc_view(W2, 0, 8, 256 * j, 256)))
            out.append((lambda st: st[:, :].rearrange("p (kc n) -> p kc n", kc=8)[:, :, 256:512],
                        kc_view(W2, 0, 8, DFF + 256 * j, 256)))
        elif kind == "ffo":
            f, j = name[1], name[2]
            nk = 4 if j < 5 else 2
            W2 = ff_out[layer, f]
            out.append((lambda st: st[:, 0:nk * 1024].rearrange("p (kc n) -> p kc n", kc=nk),
                        kc_view(W2, 512 * j, nk, 0, 1024)))
        elif kind in ("q", "k", "xr", "gr", "qm", "g", "v"):
            base = {"q": OQ, "k": OK_, "xr": OXR, "gr": OGR, "qm": OQM, "g": OGT, "v": OV}[kind]
            if kind == "v":
                c0, ncol = base + 256 * name[1], 256
            elif kind == "g" or kind == "qm":
                c0, ncol = base + 512 * name[1], 512
            else:
                c0, ncol = base + 512 * name[1], (512 if name[1] == 0 else 256)
            out.append((lambda st: st[:, 0:8 * ncol].rearrange("p (kc n) -> p kc n", kc=8),
                        kc_view(w_in[layer], 0, 8, c0, ncol)))
        elif kind == "mkv":
            out.append((lambda st: st[:, :].rearrange("p (kc n) -> p kc n", kc=8),
                        kc_view(w_mem_kv[layer], 0, 8, 512 * name[1], 512)))
        elif kind == "bra":
            out.append((lambda st: st[0:64, :].rearrange("p (h n) -> p h n", h=4),
                        w_br_attn[layer].rearrange("(h p) n -> p h n", p=64)))
        elif kind == "brl":
            nk = 4 if name[1] == 0 else 2
            out.append((lambda st: st[:, 0:nk * 1024].rearrange("p (kc n) -> p kc n", kc=nk),
                        kc_view(w_br_lru[layer], 512 * name[1], nk, 0, 1024)))
        elif kind == "brm":
            out.append((lambda st: st[:, :].rearrange("p (kc n) -> p kc n", kc=4),
                        kc_view(w_br_mem[layer], 0, 4, 0, 1024)))
        elif kind == "wo":
            out.append((lambda st: st[:, :].rearrange("p (kc n) -> p kc n", kc=4),
                        kc_view(w_out[layer], 512 * name[1], 4, 0, 1024)))
        return out

    wbt = [P.token("wbp%d" % i) for i in range(2 * NPIECE)]
    sc = P.scope()
    st32 = [sc.sbuf("st32_%d" % i, [128, 4096], F32) for i in range(3)]
    st16 = [sc.sbuf("st16_%d" % i, [128, 4096], BF16) for i in range(3)]
    lst32 = sc.sbuf("lst32", [128, 3072], F32)
    zero_bf = sc.sbuf("zero_bf", [128, 1024], BF16)
    P.I("vector", "memset", zero_bf[:, :], 0.0, writes=[zero_bf])
    P.I("gpsimd", "memset", lst32[:, :], 0.0, writes=[lst32])
    for i in range(3):
        P.I("gpsimd", "memset", st32[i][:, :], 0.0, writes=[st32[i]])
    def prep_steps(layer, st32_, st16_, lst32_, cast_engs, dq, pipelined=False):
        nb = len(st32_)
        nb16 = len(st16_)

        def load(pi_, name):
            k = pi_ % nb
            if name[0] == "lru":
                s32 = lst32_
                for di in range(2):
                    for gt, wsrc in enumerate((lru_wa, lru_wx)):
                        for par in range(2):
                            base = (di * 2 + gt) * 6
                            dst = s32[64 * par:64 * par + 64, base * 128:(base + 6) * 128].rearrange(
                                "p (c j) -> p c j", c=6)[:, :, 64 * par:64 * par + 64]
                            src_ = wsrc[layer, di].rearrange("(c two) i j -> two i c j", two=2)[par]
                            P.dma("sync", dst, src_, writes=[s32], semfrom=s32)
            else:
                s32 = st32_[k]
                for dstf, src_ in piece_srcs(layer, name):
                    P.dma(dq[k % len(dq)], dstf(s32), src_, writes=[s32], semfrom=s32)

        def cast(pi_, name):
            k = pi_ % nb
            s32 = lst32_ if name[0] == "lru" else st32_[k]
            width = 3072 if name[0] == "lru" else 4096
            s16 = st16_[pi_ % nb16]
            ce = cast_engs[pi_ % len(cast_engs)]
            P.I(ce, "copy" if ce == "scalar" else "tensor_copy", s16[:, 0:width], s32[:, 0:width],
                reads=[s32], writes=[s16])

        def store(pi_, name):
            gi = layer * NPIECE + PIDX[name]
            s16 = st16_[pi_ % nb16]
            P.dma(dq[(pi_ % nb) % len(dq)], wb[gi, :, :], s16[:, :], reads=[s16], writes=[wbt[gi]], semfrom=s16)

        steps = []
        n = len(PNAMES)
        if not pipelined:
            for pi_, name in enumerate(PNAMES):
                steps.append(lambda pi_=pi_, name=name: (load(pi_, name), cast(pi_, name), store(pi_, name)))
            return steps
        for i in range(-2, n + 1):
            def step(i=i):
                if 0 <= i + 2 < n:
                    load(i + 2, PNAMES[i + 2])
                if 0 <= i < n:
                    cast(i, PNAMES[i])
                if 0 <= i - 1 < n:
                    store(i - 1, PNAMES[i - 1])
            steps.append(step)
        return steps

    for st in prep_steps(0, st32, st16, lst32, ["vector", "gpsimd", "scalar"], ["sync", "scalar", "sync"]):
        st()
    for s in range(NSEG):
        for c in range(6):
            P.dma("sync", kT[c * 128:(c + 1) * 128, s, 0:1024], zero_bf[:, 0:1024], reads=[zero_bf], semfrom=zero_bf)
            P.dma("sync", kT[c * 128:(c + 1) * 128, s, 3072:4096], zero_bf[:, 0:1024], reads=[zero_bf], semfrom=zero_bf)
            P.dma("sync", xrT[c * 128:(c + 1) * 128, s, 0:1], zero_bf[:, 0:1], reads=[zero_bf], semfrom=zero_bf)
            P.dma("sync", xrT[c * 128:(c + 1) * 128, s, SEG + 1:SEG + 4], zero_bf[:, 0:3], reads=[zero_bf], semfrom=zero_bf)
        for g in range(3):
            d = DILS[g]
            n = SEG // d
            for r in range(d):
                P.dma("sync", vG[g][s, r, 0:64, :], zero_bf[0:64, 0:256], reads=[zero_bf], semfrom=zero_bf)
                P.dma("sync", vG[g][s, r, n + 64:n + 128, :], zero_bf[0:64, 0:256], reads=[zero_bf], semfrom=zero_bf)
    sc.close()

    class WStream:
        def __init__(self, scp, nslot, hold, tag):
            self.slots = [scp.sbuf("ring%s_%d" % (tag, i), [128, 4096], BF16) for i in range(nslot)]
            self.n = nslot
            self.ahead = nslot - hold
            self.seq = []
            self.pos = 0
            self.loaded = 0

        def plan(self, gids):
            self.seq += gids

        def _issue(self, i):
            g_ = self.seq[i]
            slot = self.slots[i % self.n]
            P.dma("sync", slot[:, :], wb[g_, :, :], reads=[wbt[g_]], writes=[slot], semfrom=slot)

        def get(self, g_):
            i = self.pos
            assert self.seq[i] == g_, (self.seq[i], g_, i)
            while self.loaded < min(len(self.seq), i + 1 + self.ahead):
                self._issue(self.loaded)
                self.loaded += 1
            self.pos += 1
            return self.slots[i % self.n]

    WSH = [None]

    def gid(layer, name):
        return layer * NPIECE + PIDX[name]

    def transposes(sc_tmp, xb, xT, evac_fn=None):
        for kc in range(8):
            pb = prot.next()
            pv = pb[:, :].bitcast(BF16)
            for b in range(4):
                P.I("tensor", "transpose",
                    pv[:, b * 128:(b + 1) * 128], xb[:, b, kc * 128:(kc + 1) * 128], ident[:, :],
                    reads=[xb, ident], writes=[pb])
            if kc % 2 == 0:
                P.I("vector", "tensor_copy", xT[:, kc, :], pv[:, 0:512],
                     reads=[pb], writes=[xT])
            else:
                P.I("scalar", "copy", xT[:, kc, :], pv[:, 0:512],
                     reads=[pb], writes=[xT])

    DEFER = []

    DEFER_LN = True
    PE_CONV = False

    def drain(n=1):
        if not DEFER_LN:
            return
        for _ in range(n):
            if DEFER:
                DEFER.pop(0)()

    def flush():
        while DEFER:
            DEFER.pop(0)()

    def ln_steps(scb, z, lng, lnb, xb):
        stats = scb["stats"]
        mv = scb["mv"]
        rstd = scb["rstd"]
        nmr = scb["nmr"]
        steps = []

        def sa_(b):
            for hh in range(2):
                P.I("vector", "bn_stats", stats[:, b, hh, :], z[:, b, hh * 512:(hh + 1) * 512],
                    reads=[z], writes=[stats])
            P.I("vector", "bn_aggr", mv[:, b, :], stats[:, b, :, :].rearrange("p a s -> p (a s)"),
                reads=[stats], writes=[mv])

        def sr_():
            P.I("vector", "tensor_scalar", rstd[:, :], mv[:, :, 1], EPS2, None, ALU.add, reads=[mv], writes=[rstd])
            P.I("gpsimd", "tensor_tensor", rstd[:, :], rstd[:, :], scb["mhalf"][:, 0:4], ALU.pow,
                reads=[rstd, scb["mhalf"]], writes=[rstd])
            P.I("vector", "scalar_tensor_tensor", nmr[:, :], mv[:, :, 0], -1.0, rstd[:, :], ALU.mult, ALU.mult,
                reads=[mv, rstd], writes=[nmr])

        def sb_(b):
            P.I("scalar", "activation", z[:, b, :], z[:, b, :], AF.Identity,
                bias=nmr[:, b:b + 1], scale=rstd[:, b:b + 1], reads=[z, nmr, rstd], writes=[z])
            P.I("vector", "tensor_tensor", z[:, b, :], z[:, b, :], lng[:, :], ALU.mult,
                reads=[z, lng], writes=[z])
            P.I("vector", "tensor_tensor", z[:, b, :], z[:, b, :], lnb[:, :], ALU.add,
                reads=[z, lnb], writes=[z])
            if xb is not None:
                P.I("scalar", "copy", xb[:, b, :], z[:, b, :], reads=[z], writes=[xb])

        for b in range(4):
            steps.append(lambda b=b: sa_(b))
        steps.append(sr_)
        for b in range(4):
            steps.append(lambda b=b: sb_(b))
        return steps

    def layer_norm(scb, z, layer, li, lng, lnb, xb):
        for st in ln_steps(scb, z, lng, lnb, xb):
            st()

    def ffn_in(scb, layer, f, xT, hT):
        tth = scb["tth"]
        tv = scb["tv"]
        for j2 in range(11):
            slot = WSH[0].get(gid(layer, ("ffi", f, j2)))
            wv = slot[:, :].rearrange("p (kc n) -> p kc n", kc=8)
            for jj in range(2):
                j = 2 * j2 + jj
                pg = prot.next()
                pu = prot.next()
                for kc in range(8):
                    P.I("tensor", "matmul",
                        pg[:, :], wv[:, kc, jj * 128:(jj + 1) * 128], xT[:, kc, :], start=(kc == 0), stop=(kc == 7),
                        reads=[slot, xT], writes=[pg])
                for kc in range(8):
                    P.I("tensor", "matmul",
                        pu[:, :], wv[:, kc, 256 + jj * 128:256 + (jj + 1) * 128], xT[:, kc, :], start=(kc == 0), stop=(kc == 7),
                        reads=[slot, xT], writes=[pu])
                th = tth.next()
                v = tv.next()
                P.I("scalar", "activation", th[:, :], pg[:, :], AF.Tanh, scale=0.5,
                     reads=[pg], writes=[th])
                P.I("vector", "scalar_tensor_tensor",
                    v[:, :], th[:, :], 1.0, pg[:, :], ALU.add, ALU.mult, reads=[th, pg], writes=[v])
                P.I("vector", "scalar_tensor_tensor",
                    hT[:, j, :], v[:, :], 0.5, pu[:, :], ALU.mult, ALU.mult, reads=[v, pu], writes=[hT])
            drain(1)
    def ffn_out(scb, layer, f, xres, hT):
        for pc in range(6):
            slot = WSH[0].get(gid(layer, ("ffo", f, pc)))
            nk = 4 if pc < 5 else 2
            wv = slot[:, 0:nk * 1024].rearrange("p (kc n) -> p kc n", kc=nk)
            for kcl in range(nk):
                j = 4 * pc + kcl
                for b in range(4):
                    for hh in range(2):
                        pb = pbanks[b * 2 + hh]
                        P.I("tensor", "matmul",
                            pb[:, :], hT[:, j, b * 128:(b + 1) * 128], wv[:, kcl, hh * 512:(hh + 1) * 512],
                            start=(j == 0), stop=(j == NFF - 1), reads=[slot, hT], writes=[pb])
        for b in range(4):
            for hh in range(2):
                pb = pbanks[b * 2 + hh]
                P.I("vector", "scalar_tensor_tensor",
                    xres[:, b, hh * 512:(hh + 1) * 512], pb[:, :], C0, xres[:, b, hh * 512:(hh + 1) * 512],
                    ALU.mult, ALU.add, reads=[pb, xres], writes=[xres])

    def evac(i, out_ap, in_ap, reads, writes):
        if i % 4 != 3:
            P.I("vector", "tensor_copy", out_ap, in_ap, reads=reads, writes=writes)
        else:
            P.I("scalar", "copy", out_ap, in_ap, reads=reads, writes=writes)

    for layer in range(2):
        x_src = x_in if layer == 0 else xl
        x_dst = xl if layer == 0 else y_out
        scl = P.scope()
        def load_ln(scp, idxs):
            lg, lb = {}, {}
            for i in idxs:
                lg[i] = scp.sbuf("lng%d" % i, [128, D], F32)
                lb[i] = scp.sbuf("lnb%d" % i, [128, D], F32)
                P.dma("sync", lg[i][:, :], lng_r[layer, i], writes=[lg[i]], semfrom=lg[i])
                P.dma("sync", lb[i][:, :], lnb_r[layer, i], writes=[lb[i]], semfrom=lb[i])
            return lg, lb
        mhalf = scl.sbuf("mhalf", [128, 512], F32)
        P.I("gpsimd", "memset", mhalf[:, :], -0.5, writes=[mhalf])

        sa = P.scope()
        lng, lnb = load_ln(sa, [0])
        WSH[0] = WStream(sa, 11, 1, "A")
        WSH[0].plan([gid(layer, ("ffi", 0, j)) for j in range(11)])
        for ti in range(NT):
            seq = [("ffo", 0, j) for j in range(6)]
            if ti + 1 < NT:
                seq += [("ffi", 0, j) for j in range(11)]
            seq += [("q", 0), ("q", 1), ("k", 0), ("k", 1), ("xr", 0), ("xr", 1), ("gr", 0), ("gr", 1), ("qm", 0),
                    ("v", 0), ("v", 1), ("v", 2)]
            WSH[0].plan([gid(layer, n) for n in seq])
        xres2 = [sa.sbuf("a_x%d" % i, [128, 4, D], F32) for i in range(2)]
        xb2 = [sa.sbuf("a_xb%d" % i, [128, 4, D], BF16) for i in range(2)]
        xT2 = [sa.sbuf("a_xT%d" % i, [128, 8, TT], BF16) for i in range(2)]
        hT = sa.sbuf("a_hT", [128, NFF, TT], BF16)
        scb = {
            "stats": sa.sbuf("a_stats", [128, 4, 2, 6], F32),
            "mv": sa.sbuf("a_mv", [128, 4, 2], F32),
            "rstd": sa.sbuf("a_rstd", [128, 4], F32),
            "nmr": sa.sbuf("a_nmr", [128, 4], F32),
            "mhalf": mhalf,
            "tth": Rot([sa.sbuf("a_th%d" % i, [128, TT], F32) for i in range(2)]),
            "tv": Rot([sa.sbuf("a_tv%d" % i, [128, TT], F32) for i in range(2)]),
        }
        stg = Rot([sa.sbuf("a_stg%d" % i, [128, TT], BF16) for i in range(4)])

        def a_loadcast(ti):
            t0 = ti * TT
            xres, xb = xres2[ti % 2], xb2[ti % 2]
            P.dma("sync", xres[:, :, :], x_src[t0:t0 + TT, :].rearrange("(b p) d -> p b d", p=128),
                  writes=[xres], semfrom=xres)
            for b in range(4):
                evac(b, xb[:, b, :], xres[:, b, :], [xres], [xb])

        a_loadcast(0)
        a_loadcast(1)
        transposes(sa, xb2[0], xT2[0])
        ffn_in(scb, layer, 0, xT2[0], hT)
        for ti in range(NT):
            s = ti // TPS
            tl = (ti % TPS) * TT
            t0 = ti * TT
            xres, xb, xT = xres2[ti % 2], xb2[ti % 2], xT2[ti % 2]
            ffn_out(scb, layer, 0, xres, hT)
            DEFER.extend(ln_steps(scb, xres, lng[0], lnb[0], xb))
            if not DEFER_LN:
                flush()
            if ti + 1 < NT:
                transposes(sa, xb2[(ti + 1) % 2], xT2[(ti + 1) % 2])
                ffn_in(scb, layer, 0, xT2[(ti + 1) % 2], hT)
            flush()
            P.dma("sync", x1s[t0:t0 + TT, :].rearrange("(b p) d -> p b d", p=128), xres[:, :, :],
                  reads=[xres], semfrom=xres)
            transposes(sa, xb, xT)
            P.dma("sync", x1T[:, t0:t0 + TT].rearrange("(kc p) t -> p kc t", p=128), xT[:, :, :],
                  reads=[xT], semfrom=xT)
            ei = 0
            for (nm, npc, dst_fn) in (
                ("q", 2, lambda c: qT[c * 128:(c + 1) * 128, t0:t0 + TT]),
                ("k", 2, lambda c: kT[c * 128:(c + 1) * 128, s, 1024 + tl:1024 + tl + TT]),
                ("xr", 2, lambda c: xrT[c * 128:(c + 1) * 128, s, 1 + tl:1 + tl + TT]),
                ("gr", 2, lambda c: grT[c * 128:(c + 1) * 128, t0:t0 + TT]),
                ("qm", 1, lambda c: qmT[c * 128:(c + 1) * 128, t0:t0 + TT]),
            ):
                cbase = 0
                for pc in range(npc):
                    slot = WSH[0].get(gid(layer, (nm, pc)))
                    ncol = 512 if pc == 0 else 256
                    wv = slot[:, 0:8 * ncol].rearrange("p (kc n) -> p kc n", kc=8)
                    for cc in range(ncol // 128):
                        pb = prot.next()
                        for kc in range(8):
                            P.I("tensor", "matmul",
                                pb[:, :], wv[:, kc, cc * 128:(cc + 1) * 128], xT[:, kc, :],
                                start=(kc == 0), stop=(kc == 7), reads=[slot, xT], writes=[pb])
                        st = stg.next()
                        evac(ei, st[:, :], pb[:, :], [pb], [st])
                        ei += 1
                        P.dma("sync", dst_fn(cbase + cc), st[:, :], reads=[st], semfrom=st)
                    cbase += ncol // 128
            for g in range(3):
                d = DILS[g]
                slot = WSH[0].get(gid(layer, ("v", g)))
                wv = slot[:, 0:2048].rearrange("p (kc n) -> p kc n", kc=8)
                for blk in range(4):
                    pb = prot.next()
                    for kc in range(8):
                        if d == 1:
                            lhs = xT[:, kc, blk * 128:(blk + 1) * 128]
                        else:
                            lhs = xT[:, kc, blk:TT:4]
                        P.I("tensor", "matmul",
                            pb[:, 0:256], lhs, wv[:, kc, :], start=(kc == 0), stop=(kc == 7),
                            reads=[slot, xT], writes=[pb])
                    st = stg.next()
                    evac(ei, st[:, 0:256], pb[:, 0:256], [pb], [st])
                    ei += 1
                    if d == 1:
                        P.dma("sync", vG[0][s, 0, 64 + tl + blk * 128:64 + tl + (blk + 1) * 128, :], st[:, 0:256],
                              reads=[st], semfrom=st)
                    elif d == 4:
                        P.dma("sync", vG[1][s, blk, 64 + tl // 4:64 + tl // 4 + 128, :], st[:, 0:256],
                              reads=[st], semfrom=st)
                    else:
                        for jq in range(4):
                            P.dma("sync", vG[2][s, blk + 4 * jq, 64 + tl // 16:64 + tl // 16 + 32, :],
                                  st[jq:128:4, 0:256], reads=[st], semfrom=st)
            if ti + 2 < NT:
                a_loadcast(ti + 2)
        sa.close()

        sbp = P.scope()
        biasT = sbp.sbuf("b_bias", [128, 12, 256], F32)
        maskT = sbp.sbuf("b_mask", [128, 256], F32)
        P.dma("sync", biasT[:, :, :], biasg[:, :, :].rearrange("g p q -> p g q"), writes=[biasT], semfrom=biasT)
        P.dma("sync", maskT[:, :], maskc[:, :], writes=[maskT], semfrom=maskT)
        for gh in range(12):
            P.I("vector", "tensor_tensor", biasT[:, gh, :], biasT[:, gh, :], maskT[:, :], ALU.add,
                 reads=[biasT, maskT], writes=[biasT])
        bias8 = sbp.sbuf("b_bias8", [128, 12, 256], BF16)
        P.I("vector", "tensor_scalar", bias8[:, :, :], biasT[:, :, :], 8.0, None, ALU.mult, reads=[biasT], writes=[bias8])
        PREPQ = []
        if layer == 0:
            pst32 = [sbp.sbuf("p_st32_%d" % i, [128, 4096], F32) for i in range(3)]
            pst16 = [sbp.sbuf("p_st16_%d" % i, [128, 4096], BF16) for i in range(2)]
            plst32 = sbp.sbuf("p_lst32", [128, 3072], F32)
            P.I("gpsimd", "memset", plst32[:, :], 0.0, writes=[plst32])
            for i in range(3):
                P.I("gpsimd", "memset", pst32[i][:, :], 0.0, writes=[pst32[i]])
            PREPQ = prep_steps(1, pst32, pst16, plst32, ["scalar", "gpsimd"], ["sync"], pipelined=True)
        unit_ctr = [0]
        KTb = [sbp.sbuf("b_KT%d" % i, [128, 4096], BF16) for i in range(2)]
        QTb = [sbp.sbuf("b_QT%d" % i, [128, SEG], BF16) for i in range(2)]
        ndacc = [sbp.sbuf("b_ndacc%d" % i, [64, 2, SEG], F32) for i in range(4)]
        vts = Rot([sbp.sbuf("b_vt%d" % i, [128, 256], BF16) for i in range(4)])
        Es = Rot([sbp.sbuf("b_E%d" % i, [128, 256], BF16) for i in range(6)])
        valL = sbp.sbuf("b_valL", [128, 64], BF16)
        valR = sbp.sbuf("b_valR", [128, 64], BF16)
        ost = Rot([sbp.sbuf("b_ost%d" % i, [64, SEG], BF16) for i in range(2)])
        for s in range(NSEG):
            P.I("vector", "tensor_scalar", valL[:, :], ones_bf[:, 0:64], flags[:, 4 * s + 2:4 * s + 3], None, ALU.mult,
                 reads=[ones_bf, flags], writes=[valL])
            P.I("vector", "tensor_scalar", valR[:, :], ones_bf[:, 0:64], flags[:, 4 * s + 3:4 * s + 4], None, ALU.mult,
                 reads=[ones_bf, flags], writes=[valR])
            for hs in range(4):
                P.I("gpsimd", "memset", ndacc[hs][:, :, :], 0.0, writes=[ndacc[hs]])
            for g in range(3):
                d = DILS[g]
                n = SEG // d
                nchunk = n // 128 + 1
                for i in range(2):
                    cc = 2 * g + i
                    P.dma("sync", KTb[i][:, :], kT[cc * 128:(cc + 1) * 128, s, :], writes=[KTb[i]], semfrom=KTb[i])
                    if s > 0:
                        P.dma("sync", KTb[i][:, 0:1024], kT[cc * 128:(cc + 1) * 128, s - 1, 2048:3072], writes=[KTb[i]], semfrom=KTb[i])
                    if s < NSEG - 1:
                        P.dma("sync", KTb[i][:, 3072:4096], kT[cc * 128:(cc + 1) * 128, s + 1, 1024:2048], writes=[KTb[i]], semfrom=KTb[i])
                    P.dma("sync", QTb[i][:, :], qT[cc * 128:(cc + 1) * 128, s * SEG:(s + 1) * SEG], writes=[QTb[i]], semfrom=QTb[i])
                pend = []
                for r in range(d):
                    for c in range(nchunk):
                        vt = vts.next()
                        P.dma("sync", vt[:, :], vG[g][s, r, 128 * c:128 * c + 128, :], writes=[vt], semfrom=vt)
                        val = ones_bf
                        if c == 0:
                            if s > 0:
                                P.dma("sync", vt[0:64, :], vG[g][s - 1, r, n:n + 64, :], writes=[vt], semfrom=vt)
                            P.I("gpsimd", "tensor_scalar",
                                vt[:, :], vt[:, :], flags[:, 4 * s + 2:4 * s + 3], None, ALU.mult,
                                reads=[vt, flags], writes=[vt])
                            val = valL
                        if c == nchunk - 1:
                            if s < NSEG - 1:
                                P.dma("sync", vt[64:128, :], vG[g][s + 1, r, 64:128, :], writes=[vt], semfrom=vt)
                            P.I("gpsimd", "tensor_scalar",
                                vt[:, :], vt[:, :], flags[:, 4 * s + 3:4 * s + 4], None, ALU.mult,
                                reads=[vt, flags], writes=[vt])
                            val = valR
                        qs0 = max(128 * c - 128, 0)
                        qs1 = min(128 * c + 128, n)
                        NQ = qs1 - qs0
                        qoff = qs0 - (128 * c - 128)
                        kcol = 1024 + (128 * c - 64) * d + r
                        qcol = qs0 * d + r
                        for hs in range(4):
                            i = hs // 2
                            p0 = 64 * (hs % 2)
                            ps = prot.next()
                            lhs = KTb[i][p0:p0 + 64, kcol:kcol + 127 * d + 1:d]
                            rhs = QTb[i][p0:p0 + 64, qcol:qcol + (NQ - 1) * d + 1:d]
                            P.I("tensor", "matmul",
                                ps[:, 0:NQ], lhs, rhs, start=True, stop=False, reads=[KTb[i], QTb[i]], writes=[ps])
                            P.I("tensor", "matmul",
                                ps[:, 0:NQ], ident[:, :], bias8[:, 4 * g + hs, qoff:qoff + NQ], start=False, stop=True,
                                reads=[ident, bias8], writes=[ps])
                            E = Es.next()
                            P.I("scalar", "activation", E[:, 0:NQ], ps[:, 0:NQ], AF.Exp, scale=0.125,
                                 reads=[ps], writes=[E])

                            def stage2(vt=vt, val=val, E=E, NQ=NQ, hs=hs, qcol=qcol, d=d):
                                pn = prot.next()
                                P.I("tensor", "matmul",
                                    pn[0:64, 0:NQ], vt[:, hs * 64:(hs + 1) * 64], E[:, 0:NQ], start=True, stop=True,
                                    reads=[vt, E], writes=[pn])
                                P.I("tensor", "matmul",
                                    pn[0:64, 256:256 + NQ], val[:, 0:64], E[:, 0:NQ], start=True, stop=True,
                                    reads=[val, E], writes=[pn])
                                acc = ndacc[hs][:, :, qcol:qcol + (NQ - 1) * d + 1:d]
                                src_ = pn[0:64, :].rearrange("p (t q) -> p t q", t=2)[:, :, 0:NQ]
                                P.I("vector", "tensor_tensor", acc, src_, acc, ALU.add,
                                     reads=[pn, ndacc[hs]], writes=[ndacc[hs]])
                            pend.append(stage2)
                            if len(pend) > 2:
                                pend.pop(0)()
                            unit_ctr[0] += 1
                            if PREPQ and unit_ctr[0] % 8 == 0:
                                PREPQ.pop(0)()
                while pend:
                    pend.pop(0)()
            for hs in range(4):
                P.I("vector", "reciprocal", ndacc[hs][:, 1, :], ndacc[hs][:, 1, :], reads=[ndacc[hs]], writes=[ndacc[hs]])
                o = ost.next()
                P.I("vector", "tensor_tensor", o[:, :], ndacc[hs][:, 0, :], ndacc[hs][:, 1, :], ALU.mult,
                     reads=[ndacc[hs]], writes=[o])
                P.dma("sync", attnT[hs * 64:(hs + 1) * 64, s * SEG:(s + 1) * SEG], o[:, :], reads=[o], semfrom=o)

        while PREPQ:
            PREPQ.pop(0)()
        sbp.close()
        sbp = P.scope()
        mem32 = sbp.sbuf("b_mem32", [128, 2, D], F32)
        WSH[0] = WStream(sbp, 3, 1, "B")
        for s in range(NSEG):
            WSH[0].plan([gid(layer, ("mkv", 0)), gid(layer, ("mkv", 1))])
        memb = sbp.sbuf("b_memb", [128, 2, D], BF16)
        memT = sbp.sbuf("b_memT", [128, 8, 256], BF16)
        KmT = sbp.sbuf("b_KmT", [128, 4, 256], BF16)
        Vm = sbp.sbuf("b_Vm", [128, 2, 512], BF16)
        qmt = Rot([sbp.sbuf("b_qm%d" % i, [128, 4, TT], BF16) for i in range(2)])
        Em = Rot([sbp.sbuf("b_Em%d" % i, [128, TT], BF16) for i in range(4)])
        rden = Rot([sbp.sbuf("b_rden%d" % i, [128, TT], F32) for i in range(2)])
        xmst = Rot([sbp.sbuf("b_xmst%d" % i, [128, TT], BF16) for i in range(2)])
        for s in range(NSEG):
            P.dma("sync", mem32[:, :, :], mem_in[s].rearrange("(b p) d -> p b d", p=128), writes=[mem32], semfrom=mem32)
            for b in range(2):
                evac(b, memb[:, b, :], mem32[:, b, :], [mem32], [memb])
            for kc in range(8):
                pb = prot.next()
                pv = pb[:, :].bitcast(BF16)
                for b in range(2):
                    P.I("tensor", "transpose",
                        pv[:, b * 128:(b + 1) * 128], memb[:, b, kc * 128:(kc + 1) * 128], ident[:, :],
                        reads=[memb, ident], writes=[pb])
                evac(kc, memT[:, kc, :], pv[:, 0:256], [pb], [memT])
            slot = WSH[0].get(gid(layer, ("mkv", 0)))
            wv = slot[:, :].rearrange("p (kc n) -> p kc n", kc=8)
            for h in range(4):
                pb = prot.next()
                for kc in range(8):
                    P.I("tensor", "matmul",
                        pb[:, 0:256], wv[:, kc, h * 128:(h + 1) * 128], memT[:, kc, :], start=(kc == 0), stop=(kc == 7),
                        reads=[slot, memT], writes=[pb])
                evac(h, KmT[:, h, :], pb[:, 0:256], [pb], [KmT])
            slot = WSH[0].get(gid(layer, ("mkv", 1)))
            wv = slot[:, :].rearrange("p (kc n) -> p kc n", kc=8)
            for mc in range(2):
                pb = prot.next()
                for kc in range(8):
                    P.I("tensor", "matmul",
                        pb[:, :], memT[:, kc, mc * 128:(mc + 1) * 128], wv[:, kc, :], start=(kc == 0), stop=(kc == 7),
                        reads=[slot, memT], writes=[pb])
                evac(mc, Vm[:, mc, :], pb[:, :], [pb], [Vm])
            for tt in range(TPS):
                t0 = s * SEG + tt * TT
                qm = qmt.next()
                P.dma("sync", qm[:, :, :], qmT[:, t0:t0 + TT].rearrange("(h p) t -> p h t", p=128), writes=[qm], semfrom=qm)
                for h in range(4):
                    ems = []
                    for mc in range(2):
                        pb = prot.next()
                        P.I("tensor", "matmul",
                            pb[:, :], KmT[:, h, mc * 128:(mc + 1) * 128], qm[:, h, :], start=True, stop=True,
                            reads=[KmT, qm], writes=[pb])
                        em = Em.next()
                        P.I("scalar", "activation", em[:, :], pb[:, :], AF.Exp, scale=128.0 ** -0.5,
                             reads=[pb], writes=[em])
                        ems.append(em)
                    pn = prot.next()
                    pd = prot.next()
                    for mc in range(2):
                        P.I("tensor", "matmul",
                            pn[:, :], Vm[:, mc, h * 128:(h + 1) * 128], ems[mc][:, :], start=(mc == 0), stop=(mc == 1),
                            reads=[Vm, ems[mc]], writes=[pn])
                    for mc in range(2):
                        P.I("tensor", "matmul",
                            pd[:, :], ones_bf[:, :], ems[mc][:, :], start=(mc == 0), stop=(mc == 1),
                            reads=[ones_bf, ems[mc]], writes=[pd])
                    rd = rden.next()
                    P.I("vector", "reciprocal", rd[:, :], pd[:, :], reads=[pd], writes=[rd])
                    xo = xmst.next()
                    P.I("vector", "tensor_tensor", xo[:, :], pn[:, :], rd[:, :], ALU.mult,
                         reads=[pn, rd], writes=[xo])
                    P.dma("sync", xmT[h * 128:(h + 1) * 128, t0:t0 + TT], xo[:, :], reads=[xo], semfrom=xo)
        sbp.close()

        sl = P.scope()
        Wbd = sl.sbuf("l_W", [128, 24, 128], BF16)
        P.dma("sync", Wbd[:, :, :], wb[gid(layer, ("lru", 0)), :, 0:3072].rearrange("p (c j) -> p c j", c=24),
              reads=[wbt[gid(layer, ("lru", 0))]], writes=[Wbd], semfrom=Wbd)
        cw = sl.sbuf("l_cw", [128, 24], F32)
        cb = sl.sbuf("l_cb", [128, 6], F32)
        hba = sl.sbuf("l_hba", [128, 12], F32)
        hbx = sl.sbuf("l_hbx", [128, 12], F32)
        lam = sl.sbuf("l_lam", [128, 12], F32)
        hsc = sl.sbuf("l_hsc", [128, 12], F32)
        hsc2 = sl.sbuf("l_hsc2", [128, 12], F32)
        P.dma("sync", cw[:, :], convw_c[layer], writes=[cw], semfrom=cw)
        P.dma("sync", cb[:, :], convb_c[layer], writes=[cb], semfrom=cb)
        P.dma("sync", hba[:, :], lba_c[layer], writes=[hba], semfrom=hba)
        P.dma("sync", hbx[:, :], lbx_c[layer], writes=[hbx], semfrom=hbx)
        P.dma("sync", lam[:, :], llam_c[layer], writes=[lam], semfrom=lam)
        P.I("vector", "tensor_scalar", hba[:, :], hba[:, :], 0.5, None, ALU.mult, reads=[hba], writes=[hba])
        P.I("vector", "tensor_scalar", hbx[:, :], hbx[:, :], 0.5, None, ALU.mult, reads=[hbx], writes=[hbx])
        P.I("scalar", "activation", lam[:, :], lam[:, :], AF.Exp, scale=-1.0, reads=[lam], writes=[lam])
        P.I("scalar", "activation", lam[:, :], lam[:, :], AF.Ln, bias=1.0, reads=[lam], writes=[lam])
        P.I("vector", "tensor_scalar", hsc[:, :], lam[:, :], -4.0, None, ALU.mult, reads=[lam], writes=[hsc])
        P.I("vector", "tensor_scalar", hsc2[:, :], lam[:, :], -8.0, None, ALU.mult, reads=[lam], writes=[hsc2])
        Dg = sl.sbuf("l_Dg", [128, 24, 128], BF16)
        for idx in range(24):
            P.I("vector", "tensor_scalar", Dg[:, idx, :], ident[:, :], cw[:, idx:idx + 1], None, ALU.mult,
                reads=[ident, cw], writes=[Dg])
        carry = [sl.sbuf("l_carry%d" % i, [128, 6], F32) for i in range(2)]
        for i in range(2):
            P.I("vector", "memset", carry[i][:, :], 0.0, writes=[carry[i]])
        xrs = Rot([sl.sbuf("l_xr%d" % i, [128, TT + 4], BF16) for i in range(8)])
        NB = 12
        xcs = Rot([sl.sbuf("l_xc%d" % i, [128, TT], F32) for i in range(7)])
        xcbs = Rot([sl.sbuf("l_xcb%d" % i, [128, TT], BF16) for i in range(7)])
        tra = Rot([sl.sbuf("l_tra%d" % i, [128, TT], F32) for i in range(3)])
        tri = Rot([sl.sbuf("l_tri%d" % i, [128, TT], F32) for i in range(NB)])
        aa = Rot([sl.sbuf("l_a%d" % i, [128, TT], F32) for i in range(NB)])
        a2 = Rot([sl.sbuf("l_a2%d" % i, [128, TT], F32) for i in range(NB)])
        uu = Rot([sl.sbuf("l_u%d" % i, [128, TT], F32) for i in range(3)])
        hh_ = Rot([sl.sbuf("l_h%d" % i, [128, TT], F32) for i in range(3)])
        hfl = Rot([sl.sbuf("l_hf%d" % i, [128, TT], F32) for i in range(3)])
        hso = Rot([sl.sbuf("l_hs%d" % i, [128, TT], BF16) for i in range(3)])

        def lru_stage1a(di, ti):
            s = ti // TPS
            tl = (ti % TPS) * TT
            keep = []
            t0 = ti * TT
            if di == 1:
                for ch in range(6):
                    xc = xcs.next()
                    P.dma("sync", xc[:, :], xcT[ch * 128:(ch + 1) * 128, t0:t0 + TT], writes=[xc], semfrom=xc)
                    xcb = xcbs.next()
                    P.I("vector", "tensor_copy", xcb[:, :], xc[:, :], reads=[xc], writes=[xcb])
                    keep.append([xc, xcb])
                return keep
            xrl = []
            for ch in range(6):
                xr = xrs.next()
                P.dma("sync", xr[:, :], xrT[ch * 128:(ch + 1) * 128, s, tl:tl + TT + 4], writes=[xr], semfrom=xr)
                xrl.append(xr)
            for ch in range(6):
                xr = xrl[ch]
                if tl == 0:
                    if s > 0:
                        P.dma("sync", xr[:, 0:1], xrT[ch * 128:(ch + 1) * 128, s - 1, SEG:SEG + 1], writes=[xr], semfrom=xr)
                    P.I("gpsimd", "tensor_scalar", xr[:, 0:1], xr[:, 0:1], flags[:, 4 * s:4 * s + 1], None, ALU.mult,
                        reads=[xr, flags], writes=[xr])
                if tl == SEG - TT:
                    if s < NSEG - 1:
                        P.dma("sync", xr[:, TT + 1:TT + 3], xrT[ch * 128:(ch + 1) * 128, s + 1, 1:3], writes=[xr], semfrom=xr)
                    P.I("gpsimd", "tensor_scalar", xr[:, TT + 1:TT + 3], xr[:, TT + 1:TT + 3], flags[:, 4 * s + 1:4 * s + 2], None, ALU.mult,
                        reads=[xr, flags], writes=[xr])
            for ch in range(6):
                xr = xrl[ch]
                if not PE_CONV:
                    xc = xcs.next()
                    P.I("vector", "tensor_scalar",
                        xc[:, :], xr[:, 0:TT], cw[:, ch * 4:ch * 4 + 1], cb[:, ch:ch + 1], ALU.mult, ALU.add,
                        reads=[xr, cw, cb], writes=[xc])
                    for j in range(1, 4):
                        P.I("vector", "scalar_tensor_tensor",
                            xc[:, :], xr[:, j:j + TT], cw[:, ch * 4 + j:ch * 4 + j + 1], xc[:, :], ALU.mult, ALU.add,
                            reads=[xr, cw, xc], writes=[xc])
                    xcb = xcbs.next()
                    P.I("scalar", "copy", xcb[:, :], xc[:, :], reads=[xc], writes=[xcb])
                    P.dma("sync", xcT[ch * 128:(ch + 1) * 128, t0:t0 + TT], xc[:, :], reads=[xc], semfrom=xc)
                    keep.append([xc, xcb])
                    continue
                pc_ = prot.next()
                for j in range(4):
                    P.I("tensor", "matmul", pc_[:, :], Dg[:, ch * 4 + j, :], xr[:, j:j + TT], start=(j == 0), stop=(j == 3),
                        reads=[Dg, xr], writes=[pc_])
                xc = xcs.next()
                P.I("scalar", "activation", xc[:, :], pc_[:, :], AF.Identity, bias=cb[:, ch:ch + 1],
                    reads=[pc_, cb], writes=[xc])
                xcb = xcbs.next()
                P.I("vector", "tensor_scalar", xcb[:, :], pc_[:, :], cb[:, ch:ch + 1], None, ALU.add,
                    reads=[pc_, cb], writes=[xcb])
                P.dma("sync", xcT[ch * 128:(ch + 1) * 128, t0:t0 + TT], xc[:, :], reads=[xc], semfrom=xc)
                keep.append([xc, xcb])
            return keep

        def lru_stage1b(di, ti, keep):
            for ch in range(6):
                xc, xcb = keep[ch]
                pa = prot.next()
                px = prot.next()
                P.I("tensor", "matmul",
                    pa[:, :], Wbd[:, (di * 2 + 0) * 6 + ch, :], xcb[:, :], start=True, stop=True,
                    reads=[Wbd, xcb], writes=[pa])
                P.I("tensor", "matmul",
                    px[:, :], Wbd[:, (di * 2 + 1) * 6 + ch, :], xcb[:, :], start=True, stop=True,
                    reads=[Wbd, xcb], writes=[px])
                ta = tra.next()
                tx = tri.next()
                col = di * 6 + ch
                P.I("scalar", "activation", ta[:, :], pa[:, :], AF.Tanh, bias=hba[:, col:col + 1], scale=0.5,
                    reads=[pa, hba], writes=[ta])
                P.I("scalar", "activation", tx[:, :], px[:, :], AF.Tanh, bias=hbx[:, col:col + 1], scale=0.5,
                    reads=[px, hbx], writes=[tx])
                a = aa.next()
                asq = a2.next()
                P.I("scalar", "activation", a[:, :], ta[:, :], AF.Exp, bias=hsc[:, col:col + 1], scale=hsc[:, col:col + 1],
                    reads=[ta, hsc], writes=[a])
                P.I("scalar", "activation", asq[:, :], ta[:, :], AF.Exp, bias=hsc2[:, col:col + 1], scale=hsc2[:, col:col + 1],
                    reads=[ta, hsc2], writes=[asq])
                P.I("scalar", "activation", asq[:, :], asq[:, :], AF.Relu, bias=1.0, scale=-1.0,
                    reads=[asq], writes=[asq])
                P.I("gpsimd", "tensor_tensor", tx[:, :], tx[:, :], xc[:, :], ALU.mult, reads=[tx, xc], writes=[tx])
                P.I("gpsimd", "tensor_tensor", tx[:, :], tx[:, :], xc[:, :], ALU.add, reads=[tx, xc], writes=[tx])
                keep[ch] += [tx, a, asq]

        def lru_stage2(di, ti, keep):
            s = ti // TPS
            t0 = ti * TT
            first_of_seg = (ti % TPS == 0) if di == 0 else (ti % TPS == TPS - 1)
            if first_of_seg:
                fcol = 4 * s + (0 if di == 0 else 1)
                P.I("vector", "tensor_scalar",
                    carry[di][:, :], carry[di][:, :], flags[:, fcol:fcol + 1], None, ALU.mult,
                    reads=[carry[di], flags], writes=[carry[di]])
            for ch in range(6):
                xc, xcb, tx, a, asq = keep[ch]
                P.I("scalar", "activation", asq[:, :], asq[:, :], AF.Sqrt, reads=[asq], writes=[asq])
                u = uu.next()
                P.I("vector", "scalar_tensor_tensor", u[:, :], tx[:, :], 0.5, asq[:, :], ALU.mult, ALU.mult,
                    reads=[tx, asq], writes=[u])
                h = hh_.next()
                if di == 0:
                    P.I("vector", "tensor_tensor_scan",
                        h[:, :], a[:, :], u[:, :], carry[0][:, ch:ch + 1], ALU.mult, ALU.add,
                        reads=[a, u, carry[0]], writes=[h])
                    P.I("vector", "tensor_copy", carry[0][:, ch:ch + 1], h[:, TT - 1:TT],
                        reads=[h], writes=[carry[0]])
                    P.dma("sync", hfT[ch * 128:(ch + 1) * 128, t0:t0 + TT], h[:, :], reads=[h], semfrom=h)
                else:
                    P.I("vector", "tensor_tensor_scan",
                        h[:, ::-1], a[:, ::-1], u[:, ::-1], carry[1][:, ch:ch + 1], ALU.mult, ALU.add,
                        reads=[a, u, carry[1]], writes=[h])
                    P.I("vector", "tensor_copy", carry[1][:, ch:ch + 1], h[:, 0:1],
                        reads=[h], writes=[carry[1]])
                    hf = hfl.next()
                    P.dma("sync", hf[:, :], hfT[ch * 128:(ch + 1) * 128, t0:t0 + TT], writes=[hf], semfrom=hf)
                    ho = hso.next()
                    P.I("vector", "tensor_tensor", ho[:, :], hf[:, :], h[:, :], ALU.add,
                        reads=[hf, h], writes=[ho])
                    P.dma("sync", hsT[ch * 128:(ch + 1) * 128, t0:t0 + TT], ho[:, :], reads=[ho], semfrom=ho)

        for di in range(2):
            order = list(range(NT)) if di == 0 else list(range(NT - 1, -1, -1))
            prev = None
            for ti in order:
                keep = lru_stage1a(di, ti)
                if prev is not None:
                    lru_stage2(di, prev[0], prev[1])
                lru_stage1b(di, ti, keep)
                prev = (ti, keep)
            lru_stage2(di, prev[0], prev[1])
            if di == 0:
                P.barrier()
        sl.close()

        scs = P.scope()
        lng, lnb = load_ln(scs, [1, 2])
        WSH[0] = WStream(scs, 7, 4, "C")
        for ti in range(NT):
            seq = [("g", 0), ("g", 1), ("bra", 0), ("g", 2), ("g", 3), ("brl", 0), ("brl", 1),
                   ("g", 4), ("g", 5), ("brm", 0), ("wo", 0), ("wo", 1)]
            seq += [("ffi", 1, j) for j in range(11)] + [("ffo", 1, j) for j in range(6)]
            WSH[0].plan([gid(layer, n) for n in seq])

        xres = scs.sbuf("c_x", [128, 4, D], F32)
        xb = scs.sbuf("c_xb", [128, 4, D], BF16)
        hT = scs.sbuf("c_hT", [128, NFF, TT], BF16)
        x1t = scs.sbuf("c_x1T", [128, 8, TT], BF16)
        xT = x1t
        at2 = [scs.sbuf("c_at%d" % i, [64, 4, TT], BF16) for i in range(2)]
        hs2 = [scs.sbuf("c_hs%d" % i, [128, 6, TT], BF16) for i in range(2)]
        gr2 = [scs.sbuf("c_gr%d" % i, [128, 6, TT], BF16) for i in range(2)]
        xm2 = [scs.sbuf("c_xm%d" % i, [128, 4, TT], BF16) for i in range(2)]
        macc = scs.sbuf("c_macc", [128, 8, TT], F32)
        mT = scs.sbuf("c_mT", [128, 8, TT], BF16)
        bgt = scs.sbuf("c_bg", [128, 24], F32)
        P.dma("sync", bgt[:, :], bgate_c[layer], writes=[bgt], semfrom=bgt)
        P.I("vector", "tensor_scalar", bgt[:, :], bgt[:, :], 0.5, None, ALU.mult, reads=[bgt], writes=[bgt])
        g1 = Rot([scs.sbuf("c_g1_%d" % i, [128, TT], F32) for i in range(2)])
        g2 = Rot([scs.sbuf("c_g2_%d" % i, [128, TT], F32) for i in range(2)])
        scb = {
            "stats": scs.sbuf("c_stats", [128, 4, 2, 6], F32),
            "mv": scs.sbuf("c_mv", [128, 4, 2], F32),
            "rstd": scs.sbuf("c_rstd", [128, 4], F32),
            "nmr": scs.sbuf("c_nmr", [128, 4], F32),
            "mhalf": mhalf,
            "tth": g1,
            "tv": g2,
        }
        def c_load_x(ti):
            t0 = ti * TT
            P.dma("sync", xres[:, :, :], x1s[t0:t0 + TT, :].rearrange("(b p) d -> p b d", p=128), writes=[xres], semfrom=xres)

        def c_load_x1t(ti):
            t0 = ti * TT
            P.dma("sync", x1t[:, :, :], x1T[:, t0:t0 + TT].rearrange("(kc p) t -> p kc t", p=128), writes=[x1t], semfrom=x1t)

        def c_pre(ti):
            t0 = ti * TT
            at_, hst, grt, xmt = at2[ti % 2], hs2[ti % 2], gr2[ti % 2], xm2[ti % 2]
            P.dma("sync", at_[:, :, :], attnT[:, t0:t0 + TT].rearrange("(h p) t -> p h t", p=64), writes=[at_], semfrom=at_)
            P.dma("sync", hst[:, :, :], hsT[:, t0:t0 + TT].rearrange("(c p) t -> p c t", p=128), writes=[hst], semfrom=hst)
            P.dma("sync", grt[:, :, :], grT[:, t0:t0 + TT].rearrange("(c p) t -> p c t", p=128), writes=[grt], semfrom=grt)
            P.dma("sync", xmt[:, :, :], xmT[:, t0:t0 + TT].rearrange("(c p) t -> p c t", p=128), writes=[xmt], semfrom=xmt)
            for ch in range(6):
                DEFER.append(lambda ch=ch, grt=grt, hst=hst: c_pre_chunk(ch, grt, hst))

        def c_pre_chunk(ch, grt, hst):
            if True:
                t1 = g1.next()
                t2 = g2.next()
                P.I("gpsimd", "tensor_tensor", t1[:, :], grt[:, ch, :], grt[:, ch, :], ALU.mult,
                     reads=[grt], writes=[t1])
                P.I("gpsimd", "tensor_scalar", t1[:, :], t1[:, :], 0.044715, 1.0, ALU.mult, ALU.add,
                     reads=[t1], writes=[t1])
                P.I("gpsimd", "tensor_tensor", t1[:, :], t1[:, :], grt[:, ch, :], ALU.mult,
                     reads=[t1, grt], writes=[t1])
                P.I("scalar", "activation", t2[:, :], t1[:, :], AF.Tanh, scale=GELU_C,
                     reads=[t1], writes=[t2])
                P.I("vector", "scalar_tensor_tensor", t2[:, :], t2[:, :], 1.0, grt[:, ch, :], ALU.add, ALU.mult,
                     reads=[t2, grt], writes=[t2])
                P.I("vector", "scalar_tensor_tensor", grt[:, ch, :], t2[:, :], 0.5, hst[:, ch, :], ALU.mult, ALU.mult,
                     reads=[t2, hst], writes=[grt])

        c_load_x(0)
        c_load_x1t(0)
        c_pre(0)
        flush()
        for ti in range(NT):
            t0 = ti * TT
            at_, hst, grt, xmt = at2[ti % 2], hs2[ti % 2], gr2[ti % 2], xm2[ti % 2]
            gslots = {}
            for br in range(3):
                gsl = [WSH[0].get(gid(layer, ("g", 2 * br))), WSH[0].get(gid(layer, ("g", 2 * br + 1)))]
                if br == 0:
                    bsl = [WSH[0].get(gid(layer, ("bra", 0)))]
                elif br == 1:
                    bsl = [WSH[0].get(gid(layer, ("brl", 0))), WSH[0].get(gid(layer, ("brl", 1)))]
                else:
                    bsl = [WSH[0].get(gid(layer, ("brm", 0)))]
                for dc in range(8):
                    pg = prot.next()
                    gs = gsl[dc // 4]
                    gv = gs[:, :].rearrange("p (kc n) -> p kc n", kc=8)
                    for kc in range(8):
                        P.I("tensor", "matmul",
                            pg[:, :], gv[:, kc, (dc % 4) * 128:(dc % 4 + 1) * 128], x1t[:, kc, :], start=(kc == 0), stop=(kc == 7),
                            reads=[gs, x1t], writes=[pg])
                    pp = prot.next()
                    if br == 0:
                        bv = bsl[0][0:64, :].rearrange("p (h n) -> p h n", h=4)
                        for h in range(4):
                            P.I("tensor", "matmul",
                                pp[:, :], bv[:, h, dc * 128:(dc + 1) * 128], at_[:, h, :], start=(h == 0), stop=(h == 3),
                                reads=[bsl[0], at_], writes=[pp])
                    elif br == 1:
                        for kc in range(6):
                            sl_ = bsl[kc // 4]
                            nk = 4 if kc < 4 else 2
                            bv = sl_[:, 0:nk * 1024].rearrange("p (kc n) -> p kc n", kc=nk)
                            P.I("tensor", "matmul",
                                pp[:, :], bv[:, kc % 4, dc * 128:(dc + 1) * 128], grt[:, kc, :], start=(kc == 0), stop=(kc == 5),
                                reads=[sl_, grt], writes=[pp])
                    else:
                        bv = bsl[0][:, :].rearrange("p (kc n) -> p kc n", kc=4)
                        for kc in range(4):
                            P.I("tensor", "matmul",
                                pp[:, :], bv[:, kc, dc * 128:(dc + 1) * 128], xmt[:, kc, :], start=(kc == 0), stop=(kc == 3),
                                reads=[bsl[0], xmt], writes=[pp])
                    drain(1)
                    th = g1.next()
                    col = br * 8 + dc
                    P.I("scalar", "activation", th[:, :], pg[:, :], AF.Tanh, bias=bgt[:, col:col + 1], scale=0.5,
                         reads=[pg, bgt], writes=[th])
                    if br == 0:
                        P.I("vector", "scalar_tensor_tensor",
                            macc[:, dc, :], th[:, :], 1.0, pp[:, :], ALU.add, ALU.mult, reads=[th, pp], writes=[macc])
                    else:
                        tv_ = g2.next()
                        P.I("vector", "scalar_tensor_tensor",
                            tv_[:, :], th[:, :], 1.0, pp[:, :], ALU.add, ALU.mult, reads=[th, pp], writes=[tv_])
                        if br == 1:
                            P.I("gpsimd", "tensor_tensor", macc[:, dc, :], macc[:, dc, :], tv_[:, :], ALU.add,
                                 reads=[macc, tv_], writes=[macc])
                        else:
                            P.I("gpsimd", "tensor_tensor", mT[:, dc, :], macc[:, dc, :], tv_[:, :], ALU.add,
                                 reads=[macc, tv_], writes=[mT])
            flush()
            wo = [WSH[0].get(gid(layer, ("wo", 0))), WSH[0].get(gid(layer, ("wo", 1)))]
            for b in range(4):
                for hh in range(2):
                    pb = prot.next()
                    for kc in range(8):
                        sl_ = wo[kc // 4]
                        wv = sl_[:, :].rearrange("p (kc n) -> p kc n", kc=4)
                        P.I("tensor", "matmul",
                            pb[:, :], mT[:, kc, b * 128:(b + 1) * 128], wv[:, kc % 4, hh * 512:(hh + 1) * 512],
                            start=(kc == 0), stop=(kc == 7), reads=[sl_, mT], writes=[pb])
                    P.I("vector", "scalar_tensor_tensor",
                        xres[:, b, hh * 512:(hh + 1) * 512], pb[:, :], C0, xres[:, b, hh * 512:(hh + 1) * 512],
                        ALU.mult, ALU.add, reads=[pb, xres], writes=[xres])
            layer_norm(scb, xres, layer, 1, lng[1], lnb[1], xb)
            transposes(scs, xb, xT)
            if ti + 1 < NT:
                c_pre(ti + 1)
            ffn_in(scb, layer, 1, xT, hT)
            flush()
            if ti + 1 < NT:
                c_load_x1t(ti + 1)
            ffn_out(scb, layer, 1, xres, hT)
            DEFER.extend(ln_steps(scb, xres, lng[2], lnb[2], None))
            DEFER.append(lambda t0=t0: P.dma("sync", x_dst[t0:t0 + TT, :].rearrange("(b p) d -> p b d", p=128),
                                             xres[:, :, :], reads=[xres], semfrom=xres))
            if ti + 1 < NT:
                DEFER.append(lambda ti=ti: c_load_x(ti + 1))
            if not DEFER_LN:
                flush()
        flush()
        scs.close()
        scl.close()

    P.barrier()
    P.emit()
    return nc, P


def _prep_inputs(inp, NSEG, core_seg_specs):
    f32 = np.float32
    rel_bias = np.asarray(inp["rel_bias"], f32)

    def t5_bucket(rel):
        half = 16
        max_exact = 8
        sign = (rel > 0).astype(np.int32) * half
        n = np.abs(rel)
        large = max_exact + (np.log(np.maximum(n, 1) / max_exact) / math.log(1024 / max_exact)
                             * (half - max_exact)).astype(np.int32)
        large = np.minimum(large, half - 1)
        return sign + np.where(n < max_exact, n, large)

    kk = np.arange(128)[:, None]
    qq = np.arange(256)[None, :]
    delta = kk - qq + 64
    biasg = np.zeros((12, 128, 256), f32)
    for g, d in enumerate(DILS):
        bk = t5_bucket(delta * d)
        for hs in range(4):
            biasg[g * 4 + hs] = rel_bias[bk, g * 4 + hs]
    maskc = np.where(np.abs(delta) <= 64, 0.0, -1e30).astype(f32)
    ident = np.eye(128, dtype=np.float32).astype(ml_dtypes.bfloat16)

    def colmajor(v, nchunk):
        return v

    b_gate = np.asarray(inp["b_gate"], f32)
    bgate_c = b_gate.reshape(2, 3, 8, 128).transpose(0, 3, 1, 2).reshape(2, 128, 24)
    conv_w = np.asarray(inp["conv_w"], f32)
    convw_c = conv_w.reshape(2, 4, 6, 128).transpose(0, 3, 2, 1).reshape(2, 128, 24)
    conv_b = np.asarray(inp["conv_b"], f32)
    convb_c = conv_b.reshape(2, 6, 128).transpose(0, 2, 1)

    def dirvec(v):
        return np.asarray(v, f32).reshape(2, 2, 6, 128).transpose(0, 3, 1, 2).reshape(2, 128, 12)

    lng_r = np.broadcast_to(np.asarray(inp["ln_g"], f32)[:, :, None, :], (2, 3, 128, D))
    lnb_r = np.broadcast_to(np.asarray(inp["ln_b"], f32)[:, :, None, :], (2, 3, 128, D))
    shared = {
        "w_in": np.asarray(inp["w_in"], f32), "ff_in": np.asarray(inp["ff_in"], f32),
        "ff_out": np.asarray(inp["ff_out"], f32), "w_mem_kv": np.asarray(inp["w_mem_kv"], f32),
        "w_br_attn": np.asarray(inp["w_br_attn"], f32), "w_br_lru": np.asarray(inp["w_br_lru"], f32),
        "w_br_mem": np.asarray(inp["w_br_mem"], f32), "w_out": np.asarray(inp["w_out"], f32),
        "lru_wa": np.asarray(inp["lru_wa"], f32), "lru_wx": np.asarray(inp["lru_wx"], f32),
        "bgate_c": np.ascontiguousarray(bgate_c), "convw_c": np.ascontiguousarray(convw_c),
        "convb_c": np.ascontiguousarray(convb_c), "lba_c": np.ascontiguousarray(dirvec(inp["lru_ba"])),
        "lbx_c": np.ascontiguousarray(dirvec(inp["lru_bx"])), "llam_c": np.ascontiguousarray(dirvec(inp["lru_lambda"])),
        "lng_r": np.ascontiguousarray(lng_r), "lnb_r": np.ascontiguousarray(lnb_r),
        "biasg": biasg, "maskc": maskc, "ident": ident,
    }
    in_maps = []
    for specs in core_seg_specs:
        xs = np.concatenate([sp[0] for sp in specs], axis=0)
        mems = np.stack([sp[1] for sp in specs], axis=0)
        fl = np.zeros((128, NSEG * 4), f32)
        for s, sp in enumerate(specs):
            fl[:, 4 * s + 0] = sp[2]
            fl[:, 4 * s + 1] = sp[3]
            fl[:64, 4 * s + 2] = sp[2]
            fl[64:, 4 * s + 2] = 1.0
            fl[:64, 4 * s + 3] = 1.0
            fl[64:, 4 * s + 3] = sp[3]
        m = dict(shared)
        m["x_in"] = np.ascontiguousarray(xs, dtype=f32)
        m["mem_in"] = np.ascontiguousarray(mems, dtype=f32)
        m["flags"] = fl
        in_maps.append(m)
    return in_maps


_CACHE = {}


def kernel(**inp):
    NSEG = 4
    xp = np.asarray(inp["x_prompt"], np.float32)
    xs = np.asarray(inp["x_sample"], np.float32)
    mp = np.asarray(inp["mem_prompt"], np.float32)
    ms = np.asarray(inp["mem_sample"], np.float32)
    specs = []
    for b in range(2):
        specs.append([(xp[b, q * SEG:(q + 1) * SEG], mp[b], 1.0 if q > 0 else 0.0, 1.0 if q < 3 else 0.0)
                      for q in range(4)])
    for c in range(2):
        specs.append([(xs[4 * c + j], ms[4 * c + j], 0.0, 0.0) for j in range(4)])
    for c in range(4):
        specs.append(specs[2 + c % 2])
    in_maps = _prep_inputs(inp, NSEG, specs)
    if "nc" not in _CACHE:
        _CACHE["nc"] = build_program(NSEG)[0]
    nc = _CACHE["nc"]
    res = run_bass_kernel_spmd(nc, in_maps, core_ids=list(range(8)))
    ys = [np.asarray(r["y"], np.float32) for r in res.results]
    y_prompt = np.stack([ys[0], ys[1]], axis=0)
    y_sample = np.concatenate([ys[2].reshape(4, SEG, D), ys[3].reshape(4, SEG, D)], axis=0)
    return (y_prompt, y_sample)
```

```python
import contextlib
import math
import numpy as np
import ml_dtypes
import concourse.bass as bass
import concourse.mybir as mybir
from concourse.bass_utils import run_bass_kernel_spmd

F32 = mybir.dt.float32
BF16 = mybir.dt.bfloat16
AF = mybir.ActivationFunctionType
ALU = mybir.AluOpType
ENGS = ("tensor", "vector", "scalar", "gpsimd", "sync")

D = 1024
DFF = 2816
NFF = 22
SEG = 2048
TT = 512
TPS = SEG // TT
ALPHA = 4.0 ** 0.25
C0 = 0.5 / ALPHA
EPS2 = 1e-5 / (ALPHA * ALPHA)
DILS = (1, 4, 16)
GELU_C = math.sqrt(2.0 / math.pi)
NPIECE = 61
OQ, OK_, OV, OXR, OGR, OQM, OGT = 0, 768, 1536, 2304, 3072, 3840, 4352


class T:
    def __init__(self, prog, h, name):
        self.p = prog
        self.h = h
        self.name = name
        self.w = []
        self.r = []
        self.ent = None

    def __getitem__(self, idx):
        return self.h[idx]


class SemEnt:
    def __init__(self, sem, i):
        self.sem = sem
        self.cnt = 0
        self.id = i


class Scope:
    def __init__(self, prog):
        self.p = prog
        self.es = contextlib.ExitStack()
        self.ts = []

    def sbuf(self, name, shape, dt):
        self.p.uid += 1
        name = "%s_u%d" % (name, self.p.uid)
        h = self.es.enter_context(self.p.nc.sbuf_tensor(name, list(shape), dt))
        t = T(self.p, h, name)
        self.ts.append(t)
        return t

    def close(self):
        self.p.barrier()
        for t in self.ts:
            if t.ent is not None:
                self.p.freeents.append(t.ent)
                t.ent = None
        self.es.close()


class Prog:
    def __init__(self, nc):
        self.nc = nc
        self.es = contextlib.ExitStack()
        self.q = {e: [] for e in ENGS}
        self.tick = {e: 0 for e in ENGS}
        self.sem = {e: self.es.enter_context(nc.semaphore("s_" + e)) for e in ENGS}
        self.known = {e: {} for e in ENGS}
        self.ents = []
        self.freeents = []
        self.ninst = 0
        self.uid = 0

    def scope(self):
        return Scope(self)

    def sbuf(self, name, shape, dt):
        self.uid += 1
        name = "%s_u%d" % (name, self.uid)
        h = self.es.enter_context(self.nc.sbuf_tensor(name, list(shape), dt))
        return T(self, h, name)

    def psum(self, name, shape, dt):
        h = self.es.enter_context(self.nc.psum_tensor(name, list(shape), dt))
        return T(self, h, name)

    def dram(self, name, shape, dt, kind=None):
        if kind is None:
            h = self.nc.dram_tensor(name, list(shape), dt)
        else:
            h = self.nc.dram_tensor(name, list(shape), dt, kind=kind)
        return T(self, h, name)

    def token(self, name):
        return T(self, None, name)

    def _ent(self, t):
        if t.ent is None:
            if self.freeents:
                t.ent = self.freeents.pop()
            else:
                s = self.es.enter_context(self.nc.semaphore("d%d" % len(self.ents)))
                t.ent = SemEnt(s, len(self.ents))
                self.ents.append(t.ent)
        return t.ent

    def _wait(self, eng, deps):
        kn = self.known[eng]
        best = {}
        for d in deps:
            if d[0] == 'e':
                if d[1] == eng and eng == 'tensor':
                    continue
                key = ('e', d[1])
                val = d[2]
                sem = self.sem[d[1]]
            else:
                key = ('d', d[3])
                val = d[2]
                sem = d[1]
            if kn.get(key, 0) >= val:
                continue
            if key not in best or best[key][1] < val:
                best[key] = (sem, val)
        for key, (sem, val) in best.items():
            kn[key] = val
            self.q[eng].append(lambda e, sem=sem, val=val: e.wait_ge(sem, val))

    def I(self, eng, name, *args, reads=(), writes=(), **kw):
        deps = []
        for b in reads:
            deps += b.w
        for b in writes:
            deps += b.w
            deps += b.r
        self._wait(eng, deps)
        self.tick[eng] += 1
        tk = self.tick[eng]
        sem = self.sem[eng]
        self.q[eng].append(lambda e, name=name, args=args, kw=kw, sem=sem: getattr(e, name)(*args, **kw).then_inc(sem, 1))
        me = ('e', eng, tk)
        for b in writes:
            b.w = [me]
            b.r = []
        for b in reads:
            if b not in writes:
                b.r.append(me)
        self.ninst += 1

    def dma(self, eng, out_ap, in_ap, reads=(), writes=(), semfrom=None):
        deps = []
        for b in reads:
            deps += b.w
        for b in writes:
            deps += b.w
            deps += b.r
        self._wait(eng, deps)
        ent = self._ent(semfrom)
        ent.cnt += 16
        sem = ent.sem
        self.q[eng].append(lambda e, o=out_ap, i=in_ap, sem=sem:
                           e.dma_start(out=o, in_=i).then_inc(sem, 16))
        me = ('d', sem, ent.cnt, ent.id)
        for b in writes:
            b.w = [me]
            b.r = []
        for b in reads:
            if b not in writes:
                b.r.append(me)
        self.ninst += 1

    def barrier(self):
        for e in ENGS:
            deps = [('e', o, self.tick[o]) for o in ENGS if self.tick[o] > 0 and not (o == e and e == 'tensor')]
            deps += [('d', en.sem, en.cnt, en.id) for en in self.ents if en.cnt > 0]
            self._wait(e, deps)

    def emit(self):
        nc = self.nc
        with nc.allow_non_contiguous_dma(reason="halo columns / strided scratch layouts"), nc.Block() as block:
            for e in ENGS:
                lst = self.q[e]
                if not lst:
                    continue

                def body(engine, lst=lst):
                    for f in lst:
                        f(engine)
                getattr(block, e)(body)
        self.es.close()


class Rot:
    def __init__(self, items):
        self.items = items
        self.i = 0

    def next(self):
        t = self.items[self.i % len(self.items)]
        self.i += 1
        return t


def piece_index():
    names = []
    for f in range(2):
        names += [("ffi", f, j) for j in range(11)]
        names += [("ffo", f, j) for j in range(6)]
    names += [("q", 0), ("q", 1), ("k", 0), ("k", 1), ("v", 0), ("v", 1), ("v", 2),
              ("xr", 0), ("xr", 1), ("gr", 0), ("gr", 1), ("qm", 0)]
    names += [("g", j) for j in range(6)]
    names += [("mkv", 0), ("mkv", 1), ("bra", 0), ("brl", 0), ("brl", 1), ("brm", 0), ("wo", 0), ("wo", 1),
              ("lru", 0)]
    assert len(names) == NPIECE
    return {n: i for i, n in enumerate(names)}, names


PIDX, PNAMES = piece_index()


def build_program(NSEG, dbg=False):
    NTOK = NSEG * SEG
    NT = NTOK // TT
    nc = bass.Bass("TRN2", target_bir_lowering=False)
    P = Prog(nc)

    def din(name, shape, dt=F32):
        return P.dram(name, shape, dt, kind="ExternalInput")

    x_in = din("x_in", [NTOK, D])
    mem_in = din("mem_in", [NSEG, 256, D])
    flags_in = din("flags", [128, NSEG * 4])
    w_in = din("w_in", [2, D, 7424])
    ff_in = din("ff_in", [2, 2, D, 2 * DFF])
    ff_out = din("ff_out", [2, 2, DFF, D])
    w_mem_kv = din("w_mem_kv", [2, D, 1024])
    w_br_attn = din("w_br_attn", [2, 256, D])
    w_br_lru = din("w_br_lru", [2, 768, D])
    w_br_mem = din("w_br_mem", [2, 512, D])
    w_out = din("w_out", [2, D, D])
    lru_wa = din("lru_wa", [2, 2, 12, 64, 64])
    lru_wx = din("lru_wx", [2, 2, 12, 64, 64])
    bgate_c = din("bgate_c", [2, 128, 24])
    convw_c = din("convw_c", [2, 128, 24])
    convb_c = din("convb_c", [2, 128, 6])
    lba_c = din("lba_c", [2, 128, 12])
    lbx_c = din("lbx_c", [2, 128, 12])
    llam_c = din("llam_c", [2, 128, 12])
    lng_r = din("lng_r", [2, 3, 128, D])
    lnb_r = din("lnb_r", [2, 3, 128, D])
    biasg = din("biasg", [12, 128, 256])
    maskc = din("maskc", [128, 256])
    ident_in = din("ident", [128, 128], BF16)
    y_out = P.dram("y", [NTOK, D], F32, kind="ExternalOutput")

    wb = P.dram("wb", [2 * NPIECE, 128, 4096], BF16)
    x1s = P.dram("x1s", [NTOK, D], F32)
    xl = P.dram("xl", [NTOK, D], F32)
    x1T = P.dram("x1T", [D, NTOK], BF16)
    qT = P.dram("qT", [768, NTOK], BF16)
    kT = P.dram("kT", [768, NSEG, 4096], BF16)
    vG = [P.dram("vG%d" % g, [NSEG, DILS[g], SEG // DILS[g] + 128, 256], BF16) for g in range(3)]
    XRW = SEG + 4
    xrT = P.dram("xrT", [768, NSEG, XRW], BF16)
    grT = P.dram("grT", [768, NTOK], BF16)
    qmT = P.dram("qmT", [512, NTOK], BF16)
    hfT = P.dram("hfT", [768, NTOK], F32)
    xcT = P.dram("xcT", [768, NTOK], F32)
    hsT = P.dram("hsT", [768, NTOK], BF16)
    attnT = P.dram("attnT", [256, NTOK], BF16)
    xmT = P.dram("xmT", [512, NTOK], BF16)
    dbg_out = {}

    ident = P.sbuf("ident", [128, 128], BF16)
    ones_bf = P.sbuf("ones_bf", [128, 128], BF16)
    flags = P.sbuf("flags_sb", [128, NSEG * 4], F32)
    pbanks = [P.psum("pb%d" % i, [128, 512], F32) for i in range(8)]
    prot = Rot(pbanks)

    P.dma("sync", ident[:, :], ident_in[:, :], writes=[ident], semfrom=ident)
    P.dma("sync", flags[:, :], flags_in[:, :], writes=[flags], semfrom=flags)
    P.I("vector", "memset", ones_bf[:, :], 1.0, writes=[ones_bf])

    def piece_srcs(layer, name):
        kind = name[0]
        out = []

        def kc_view(W2, k0, nk, c0, ncol):
            return W2[k0:k0 + nk * 128, c0:c0 + ncol].rearrange("(kc p) n -> p kc n", p=128)

        if kind == "ffi":
            f, j = name[1], name[2]
            W2 = ff_in[layer, f]
            out.append((lambda st: st[:, :].rearrange("p (kc n) -> p kc n", kc=8)[:, :, 0:256],
                        kc_view(W2, 0, 8, 256 * j, 256)))
            out.append((lambda st: st[:, :].rearrange("p (kc n) -> p kc n", kc=8)[:, :, 256:512],
                        kc_view(W2, 0, 8, DFF + 256 * j, 256)))
        elif kind == "ffo":
            f, j = name[1], name[2]
            nk = 4 if j < 5 else 2
            W2 = ff_out[layer, f]
            out.append((lambda st: st[:, 0:nk * 1024].rearrange("p (kc n) -> p kc n", kc=nk),
                        kc_view(W2, 512 * j, nk, 0, 1024)))
        elif kind in ("q", "k", "xr", "gr", "qm", "g", "v"):
            base = {"q": OQ, "k": OK_, "xr": OXR, "gr": OGR, "qm": OQM, "g": OGT, "v": OV}[kind]
            if kind == "v":
                c0, ncol = base + 256 * name[1], 256
            elif kind == "g" or kind == "qm":
                c0, ncol = base + 512 * name[1], 512
            else:
                c0, ncol = base + 512 * name[1], (512 if name[1] == 0 else 256)
            out.append((lambda st: st[:, 0:8 * ncol].rearrange("p (kc n) -> p kc n", kc=8),
                        kc_view(w_in[layer], 0, 8, c0, ncol)))
        elif kind == "mkv":
            out.append((lambda st: st[:, :].rearrange("p (kc n) -> p kc n", kc=8),
                        kc_view(w_mem_kv[layer], 0, 8, 512 * name[1], 512)))
        elif kind == "bra":
            out.append((lambda st: st[0:64, :].rearrange("p (h n) -> p h n", h=4),
                        w_br_attn[layer].rearrange("(h p) n -> p h n", p=64)))
        elif kind == "brl":
            nk = 4 if name[1] == 0 else 2
            out.append((lambda st: st[:, 0:nk * 1024].rearrange("p (kc n) -> p kc n", kc=nk),
                        kc_view(w_br_lru[layer], 512 * name[1], nk, 0, 1024)))
        elif kind == "brm":
            out.append((lambda st: st[:, :].rearrange("p (kc n) -> p kc n", kc=4),
                        kc_view(w_br_mem[layer], 0, 4, 0, 1024)))
        elif kind == "wo":
            out.append((lambda st: st[:, :].rearrange("p (kc n) -> p kc n", kc=4),
                        kc_view(w_out[layer], 512 * name[1], 4, 0, 1024)))
        return out

    wbt = [P.token("wbp%d" % i) for i in range(2 * NPIECE)]
    sc = P.scope()
    st32 = [sc.sbuf("st32_%d" % i, [128, 4096], F32) for i in range(3)]
    st16 = [sc.sbuf("st16_%d" % i, [128, 4096], BF16) for i in range(3)]
    lst32 = sc.sbuf("lst32", [128, 3072], F32)
    zero_bf = sc.sbuf("zero_bf", [128, 1024], BF16)
    P.I("vector", "memset", zero_bf[:, :], 0.0, writes=[zero_bf])
    P.I("gpsimd", "memset", lst32[:, :], 0.0, writes=[lst32])
    for i in range(3):
        P.I("gpsimd", "memset", st32[i][:, :], 0.0, writes=[st32[i]])
    def prep_steps(layer, st32_, st16_, lst32_, cast_engs, dq, pipelined=False):
        nb = len(st32_)
        nb16 = len(st16_)

        def load(pi_, name):
            k = pi_ % nb
            if name[0] == "lru":
                s32 = lst32_
                for di in range(2):
                    for gt, wsrc in enumerate((lru_wa, lru_wx)):
                        for par in range(2):
                            base = (di * 2 + gt) * 6
                            dst = s32[64 * par:64 * par + 64, base * 128:(base + 6) * 128].rearrange(
                                "p (c j) -> p c j", c=6)[:, :, 64 * par:64 * par + 64]
                            src_ = wsrc[layer, di].rearrange("(c two) i j -> two i c j", two=2)[par]
                            P.dma("sync", dst, src_, writes=[s32], semfrom=s32)
            else:
                s32 = st32_[k]
                for dstf, src_ in piece_srcs(layer, name):
                    P.dma(dq[k % len(dq)], dstf(s32), src_, writes=[s32], semfrom=s32)

        def cast(pi_, name):
            k = pi_ % nb
            s32 = lst32_ if name[0] == "lru" else st32_[k]
            width = 3072 if name[0] == "lru" else 4096
            s16 = st16_[pi_ % nb16]
            ce = cast_engs[pi_ % len(cast_engs)]
            P.I(ce, "copy" if ce == "scalar" else "tensor_copy", s16[:, 0:width], s32[:, 0:width],
                reads=[s32], writes=[s16])

        def store(pi_, name):
            gi = layer * NPIECE + PIDX[name]
            s16 = st16_[pi_ % nb16]
            P.dma(dq[(pi_ % nb) % len(dq)], wb[gi, :, :], s16[:, :], reads=[s16], writes=[wbt[gi]], semfrom=s16)

        steps = []
        n = len(PNAMES)
        if not pipelined:
            for pi_, name in enumerate(PNAMES):
                steps.append(lambda pi_=pi_, name=name: (load(pi_, name), cast(pi_, name), store(pi_, name)))
            return steps
        for i in range(-2, n + 1):
            def step(i=i):
                if 0 <= i + 2 < n:
                    load(i + 2, PNAMES[i + 2])
                if 0 <= i < n:
                    cast(i, PNAMES[i])
                if 0 <= i - 1 < n:
                    store(i - 1, PNAMES[i - 1])
            steps.append(step)
        return steps

    for st in prep_steps(0, st32, st16, lst32, ["vector", "gpsimd", "scalar"], ["sync", "scalar", "sync"]):
        st()
    for s in range(NSEG):
        for c in range(6):
            P.dma("sync", kT[c * 128:(c + 1) * 128, s, 0:1024], zero_bf[:, 0:1024], reads=[zero_bf], semfrom=zero_bf)
            P.dma("sync", kT[c * 128:(c + 1) * 128, s, 3072:4096], zero_bf[:, 0:1024], reads=[zero_bf], semfrom=zero_bf)
            P.dma("sync", xrT[c * 128:(c + 1) * 128, s, 0:1], zero_bf[:, 0:1], reads=[zero_bf], semfrom=zero_bf)
            P.dma("sync", xrT[c * 128:(c + 1) * 128, s, SEG + 1:SEG + 4], zero_bf[:, 0:3], reads=[zero_bf], semfrom=zero_bf)
        for g in range(3):
            d = DILS[g]
            n = SEG // d
            for r in range(d):
                P.dma("sync", vG[g][s, r, 0:64, :], zero_bf[0:64, 0:256], reads=[zero_bf], semfrom=zero_bf)
                P.dma("sync", vG[g][s, r, n + 64:n + 128, :], zero_bf[0:64, 0:256], reads=[zero_bf], semfrom=zero_bf)
    sc.close()

    class WStream:
        def __init__(self, scp, nslot, hold, tag):
            self.slots = [scp.sbuf("ring%s_%d" % (tag, i), [128, 4096], BF16) for i in range(nslot)]
            self.n = nslot
            self.ahead = nslot - hold
            self.seq = []
            self.pos = 0
            self.loaded = 0

        def plan(self, gids):
            self.seq += gids

        def _issue(self, i):
            g_ = self.seq[i]
            slot = self.slots[i % self.n]
            P.dma("sync", slot[:, :], wb[g_, :, :], reads=[wbt[g_]], writes=[slot], semfrom=slot)

        def get(self, g_):
            i = self.pos
            assert self.seq[i] == g_, (self.seq[i], g_, i)
            while self.loaded < min(len(self.seq), i + 1 + self.ahead):
                self._issue(self.loaded)
                self.loaded += 1
            self.pos += 1
            return self.slots[i % self.n]

    WSH = [None]

    def gid(layer, name):
        return layer * NPIECE + PIDX[name]

    def transposes(sc_tmp, xb, xT, evac_fn=None):
        for kc in range(8):
            pb = prot.next()
            pv = pb[:, :].bitcast(BF16)
            for b in range(4):
                P.I("tensor", "transpose",
                    pv[:, b * 128:(b + 1) * 128], xb[:, b, kc * 128:(kc + 1) * 128], ident[:, :],
                    reads=[xb, ident], writes=[pb])
            if kc % 2 == 0:
                P.I("vector", "tensor_copy", xT[:, kc, :], pv[:, 0:512],
                     reads=[pb], writes=[xT])
            else:
                P.I("scalar", "copy", xT[:, kc, :], pv[:, 0:512],
                     reads=[pb], writes=[xT])

    DEFER = []

    DEFER_LN = True
    PE_CONV = False

    def drain(n=1):
        if not DEFER_LN:
            return
        for _ in range(n):
            if DEFER:
                DEFER.pop(0)()

    def flush():
        while DEFER:
            DEFER.pop(0)()

    def ln_steps(scb, z, lng, lnb, xb):
        stats = scb["stats"]
        mv = scb["mv"]
        rstd = scb["rstd"]
        nmr = scb["nmr"]
        steps = []

        def sa_(b):
            for hh in range(2):
                P.I("vector", "bn_stats", stats[:, b, hh, :], z[:, b, hh * 512:(hh + 1) * 512],
                    reads=[z], writes=[stats])
            P.I("vector", "bn_aggr", mv[:, b, :], stats[:, b, :, :].rearrange("p a s -> p (a s)"),
                reads=[stats], writes=[mv])

        def sr_():
            P.I("vector", "tensor_scalar", rstd[:, :], mv[:, :, 1], EPS2, None, ALU.add, reads=[mv], writes=[rstd])
            P.I("gpsimd", "tensor_tensor", rstd[:, :], rstd[:, :], scb["mhalf"][:, 0:4], ALU.pow,
                reads=[rstd, scb["mhalf"]], writes=[rstd])
            P.I("vector", "scalar_tensor_tensor", nmr[:, :], mv[:, :, 0], -1.0, rstd[:, :], ALU.mult, ALU.mult,
                reads=[mv, rstd], writes=[nmr])

        def sb_(b):
            P.I("scalar", "activation", z[:, b, :], z[:, b, :], AF.Identity,
                bias=nmr[:, b:b + 1], scale=rstd[:, b:b + 1], reads=[z, nmr, rstd], writes=[z])
            P.I("vector", "tensor_tensor", z[:, b, :], z[:, b, :], lng[:, :], ALU.mult,
                reads=[z, lng], writes=[z])
            P.I("vector", "tensor_tensor", z[:, b, :], z[:, b, :], lnb[:, :], ALU.add,
                reads=[z, lnb], writes=[z])
            if xb is not None:
                P.I("scalar", "copy", xb[:, b, :], z[:, b, :], reads=[z], writes=[xb])

        for b in range(4):
            steps.append(lambda b=b: sa_(b))
        steps.append(sr_)
        for b in range(4):
            steps.append(lambda b=b: sb_(b))
        return steps

    def layer_norm(scb, z, layer, li, lng, lnb, xb):
        for st in ln_steps(scb, z, lng, lnb, xb):
            st()

    def ffn_in(scb, layer, f, xT, hT):
        tth = scb["tth"]
        tv = scb["tv"]
        for j2 in range(11):
            slot = WSH[0].get(gid(layer, ("ffi", f, j2)))
            wv = slot[:, :].rearrange("p (kc n) -> p kc n", kc=8)
            for jj in range(2):
                j = 2 * j2 + jj
                pg = prot.next()
                pu = prot.next()
                for kc in range(8):
                    P.I("tensor", "matmul",
                        pg[:, :], wv[:, kc, jj * 128:(jj + 1) * 128], xT[:, kc, :], start=(kc == 0), stop=(kc == 7),
                        reads=[slot, xT], writes=[pg])
                for kc in range(8):
                    P.I("tensor", "matmul",
                        pu[:, :], wv[:, kc, 256 + jj * 128:256 + (jj + 1) * 128], xT[:, kc, :], start=(kc == 0), stop=(kc == 7),
                        reads=[slot, xT], writes=[pu])
                th = tth.next()
                v = tv.next()
                P.I("scalar", "activation", th[:, :], pg[:, :], AF.Tanh, scale=0.5,
                     reads=[pg], writes=[th])
                P.I("vector", "scalar_tensor_tensor",
                    v[:, :], th[:, :], 1.0, pg[:, :], ALU.add, ALU.mult, reads=[th, pg], writes=[v])
                P.I("vector", "scalar_tensor_tensor",
                    hT[:, j, :], v[:, :], 0.5, pu[:, :], ALU.mult, ALU.mult, reads=[v, pu], writes=[hT])
            drain(1)
    def ffn_out(scb, layer, f, xres, hT):
        for pc in range(6):
            slot = WSH[0].get(gid(layer, ("ffo", f, pc)))
            nk = 4 if pc < 5 else 2
            wv = slot[:, 0:nk * 1024].rearrange("p (kc n) -> p kc n", kc=nk)
            for kcl in range(nk):
                j = 4 * pc + kcl
                for b in range(4):
                    for hh in range(2):
                        pb = pbanks[b * 2 + hh]
                        P.I("tensor", "matmul",
                            pb[:, :], hT[:, j, b * 128:(b + 1) * 128], wv[:, kcl, hh * 512:(hh + 1) * 512],
                            start=(j == 0), stop=(j == NFF - 1), reads=[slot, hT], writes=[pb])
        for b in range(4):
            for hh in range(2):
                pb = pbanks[b * 2 + hh]
                P.I("vector", "scalar_tensor_tensor",
                    xres[:, b, hh * 512:(hh + 1) * 512], pb[:, :], C0, xres[:, b, hh * 512:(hh + 1) * 512],
                    ALU.mult, ALU.add, reads=[pb, xres], writes=[xres])

    def evac(i, out_ap, in_ap, reads, writes):
        if i % 2 == 0:
            P.I("vector", "tensor_copy", out_ap, in_ap, reads=reads, writes=writes)
        else:
            P.I("scalar", "copy", out_ap, in_ap, reads=reads, writes=writes)

    for layer in range(2):
        x_src = x_in if layer == 0 else xl
        x_dst = xl if layer == 0 else y_out
        scl = P.scope()
        def load_ln(scp, idxs):
            lg, lb = {}, {}
            for i in idxs:
                lg[i] = scp.sbuf("lng%d" % i, [128, D], F32)
                lb[i] = scp.sbuf("lnb%d" % i, [128, D], F32)
                P.dma("sync", lg[i][:, :], lng_r[layer, i], writes=[lg[i]], semfrom=lg[i])
                P.dma("sync", lb[i][:, :], lnb_r[layer, i], writes=[lb[i]], semfrom=lb[i])
            return lg, lb
        mhalf = scl.sbuf("mhalf", [128, 512], F32)
        P.I("gpsimd", "memset", mhalf[:, :], -0.5, writes=[mhalf])

        sa = P.scope()
        lng, lnb = load_ln(sa, [0])
        WSH[0] = WStream(sa, 11, 1, "A")
        WSH[0].plan([gid(layer, ("ffi", 0, j)) for j in range(11)])
        for ti in range(NT):
            seq = [("ffo", 0, j) for j in range(6)]
            if ti + 1 < NT:
                seq += [("ffi", 0, j) for j in range(11)]
            seq += [("q", 0), ("q", 1), ("k", 0), ("k", 1), ("xr", 0), ("xr", 1), ("gr", 0), ("gr", 1), ("qm", 0),
                    ("v", 0), ("v", 1), ("v", 2)]
            WSH[0].plan([gid(layer, n) for n in seq])
        xres2 = [sa.sbuf("a_x%d" % i, [128, 4, D], F32) for i in range(2)]
        xb2 = [sa.sbuf("a_xb%d" % i, [128, 4, D], BF16) for i in range(2)]
        xT2 = [sa.sbuf("a_xT%d" % i, [128, 8, TT], BF16) for i in range(2)]
        hT = sa.sbuf("a_hT", [128, NFF, TT], BF16)
        scb = {
            "stats": sa.sbuf("a_stats", [128, 4, 2, 6], F32),
            "mv": sa.sbuf("a_mv", [128, 4, 2], F32),
            "rstd": sa.sbuf("a_rstd", [128, 4], F32),
            "nmr": sa.sbuf("a_nmr", [128, 4], F32),
            "mhalf": mhalf,
            "tth": Rot([sa.sbuf("a_th%d" % i, [128, TT], F32) for i in range(2)]),
            "tv": Rot([sa.sbuf("a_tv%d" % i, [128, TT], F32) for i in range(2)]),
        }
        stg = Rot([sa.sbuf("a_stg%d" % i, [128, TT], BF16) for i in range(4)])

        def a_loadcast(ti):
            t0 = ti * TT
            xres, xb = xres2[ti % 2], xb2[ti % 2]
            P.dma("sync", xres[:, :, :], x_src[t0:t0 + TT, :].rearrange("(b p) d -> p b d", p=128),
                  writes=[xres], semfrom=xres)
            for b in range(4):
                evac(b, xb[:, b, :], xres[:, b, :], [xres], [xb])

        a_loadcast(0)
        a_loadcast(1)
        transposes(sa, xb2[0], xT2[0])
        ffn_in(scb, layer, 0, xT2[0], hT)
        for ti in range(NT):
            s = ti // TPS
            tl = (ti % TPS) * TT
            t0 = ti * TT
            xres, xb, xT = xres2[ti % 2], xb2[ti % 2], xT2[ti % 2]
            ffn_out(scb, layer, 0, xres, hT)
            DEFER.extend(ln_steps(scb, xres, lng[0], lnb[0], xb))
            if not DEFER_LN:
                flush()
            if ti + 1 < NT:
                transposes(sa, xb2[(ti + 1) % 2], xT2[(ti + 1) % 2])
                ffn_in(scb, layer, 0, xT2[(ti + 1) % 2], hT)
            flush()
            P.dma("sync", x1s[t0:t0 + TT, :].rearrange("(b p) d -> p b d", p=128), xres[:, :, :],
                  reads=[xres], semfrom=xres)
            transposes(sa, xb, xT)
            P.dma("sync", x1T[:, t0:t0 + TT].rearrange("(kc p) t -> p kc t", p=128), xT[:, :, :],
                  reads=[xT], semfrom=xT)
            ei = 0
            for (nm, npc, dst_fn) in (
                ("q", 2, lambda c: qT[c * 128:(c + 1) * 128, t0:t0 + TT]),
                ("k", 2, lambda c: kT[c * 128:(c + 1) * 128, s, 1024 + tl:1024 + tl + TT]),
                ("xr", 2, lambda c: xrT[c * 128:(c + 1) * 128, s, 1 + tl:1 + tl + TT]),
                ("gr", 2, lambda c: grT[c * 128:(c + 1) * 128, t0:t0 + TT]),
                ("qm", 1, lambda c: qmT[c * 128:(c + 1) * 128, t0:t0 + TT]),
            ):
                cbase = 0
                for pc in range(npc):
                    slot = WSH[0].get(gid(layer, (nm, pc)))
                    ncol = 512 if pc == 0 else 256
                    wv = slot[:, 0:8 * ncol].rearrange("p (kc n) -> p kc n", kc=8)
                    for cc in range(ncol // 128):
                        pb = prot.next()
                        for kc in range(8):
                            P.I("tensor", "matmul",
                                pb[:, :], wv[:, kc, cc * 128:(cc + 1) * 128], xT[:, kc, :],
                                start=(kc == 0), stop=(kc == 7), reads=[slot, xT], writes=[pb])
                        st = stg.next()
                        evac(ei, st[:, :], pb[:, :], [pb], [st])
                        ei += 1
                        P.dma("sync", dst_fn(cbase + cc), st[:, :], reads=[st], semfrom=st)
                    cbase += ncol // 128
            for g in range(3):
                d = DILS[g]
                slot = WSH[0].get(gid(layer, ("v", g)))
                wv = slot[:, 0:2048].rearrange("p (kc n) -> p kc n", kc=8)
                for blk in range(4):
                    pb = prot.next()
                    for kc in range(8):
                        if d == 1:
                            lhs = xT[:, kc, blk * 128:(blk + 1) * 128]
                        else:
                            lhs = xT[:, kc, blk:TT:4]
                        P.I("tensor", "matmul",
                            pb[:, 0:256], lhs, wv[:, kc, :], start=(kc == 0), stop=(kc == 7),
                            reads=[slot, xT], writes=[pb])
                    st = stg.next()
                    evac(ei, st[:, 0:256], pb[:, 0:256], [pb], [st])
                    ei += 1
                    if d == 1:
                        P.dma("sync", vG[0][s, 0, 64 + tl + blk * 128:64 + tl + (blk + 1) * 128, :], st[:, 0:256],
                              reads=[st], semfrom=st)
                    elif d == 4:
                        P.dma("sync", vG[1][s, blk, 64 + tl // 4:64 + tl // 4 + 128, :], st[:, 0:256],
                              reads=[st], semfrom=st)
                    else:
                        for jq in range(4):
                            P.dma("sync", vG[2][s, blk + 4 * jq, 64 + tl // 16:64 + tl // 16 + 32, :],
                                  st[jq:128:4, 0:256], reads=[st], semfrom=st)
            if ti + 2 < NT:
                a_loadcast(ti + 2)
        sa.close()

        sbp = P.scope()
        biasT = sbp.sbuf("b_bias", [128, 12, 256], F32)
        maskT = sbp.sbuf("b_mask", [128, 256], F32)
        P.dma("sync", biasT[:, :, :], biasg[:, :, :].rearrange("g p q -> p g q"), writes=[biasT], semfrom=biasT)
        P.dma("sync", maskT[:, :], maskc[:, :], writes=[maskT], semfrom=maskT)
        for gh in range(12):
            P.I("vector", "tensor_tensor", biasT[:, gh, :], biasT[:, gh, :], maskT[:, :], ALU.add,
                 reads=[biasT, maskT], writes=[biasT])
        bias8 = sbp.sbuf("b_bias8", [128, 12, 256], BF16)
        P.I("vector", "tensor_scalar", bias8[:, :, :], biasT[:, :, :], 8.0, None, ALU.mult, reads=[biasT], writes=[bias8])
        PREPQ = []
        if layer == 0:
            pst32 = [sbp.sbuf("p_st32_%d" % i, [128, 4096], F32) for i in range(3)]
            pst16 = [sbp.sbuf("p_st16_%d" % i, [128, 4096], BF16) for i in range(2)]
            plst32 = sbp.sbuf("p_lst32", [128, 3072], F32)
            P.I("gpsimd", "memset", plst32[:, :], 0.0, writes=[plst32])
            for i in range(3):
                P.I("gpsimd", "memset", pst32[i][:, :], 0.0, writes=[pst32[i]])
            PREPQ = prep_steps(1, pst32, pst16, plst32, ["scalar", "gpsimd"], ["sync"], pipelined=True)
        unit_ctr = [0]
        KTb = [sbp.sbuf("b_KT%d" % i, [128, 4096], BF16) for i in range(2)]
        QTb = [sbp.sbuf("b_QT%d" % i, [128, SEG], BF16) for i in range(2)]
        ndacc = [sbp.sbuf("b_ndacc%d" % i, [64, 2, SEG], F32) for i in range(4)]
        vts = Rot([sbp.sbuf("b_vt%d" % i, [128, 256], BF16) for i in range(4)])
        Es = Rot([sbp.sbuf("b_E%d" % i, [128, 256], BF16) for i in range(6)])
        valL = sbp.sbuf("b_valL", [128, 64], BF16)
        valR = sbp.sbuf("b_valR", [128, 64], BF16)
        ost = Rot([sbp.sbuf("b_ost%d" % i, [64, SEG], BF16) for i in range(2)])
        for s in range(NSEG):
            P.I("vector", "tensor_scalar", valL[:, :], ones_bf[:, 0:64], flags[:, 4 * s + 2:4 * s + 3], None, ALU.mult,
                 reads=[ones_bf, flags], writes=[valL])
            P.I("vector", "tensor_scalar", valR[:, :], ones_bf[:, 0:64], flags[:, 4 * s + 3:4 * s + 4], None, ALU.mult,
                 reads=[ones_bf, flags], writes=[valR])
            for hs in range(4):
                P.I("gpsimd", "memset", ndacc[hs][:, :, :], 0.0, writes=[ndacc[hs]])
            for g in range(3):
                d = DILS[g]
                n = SEG // d
                nchunk = n // 128 + 1
                for i in range(2):
                    cc = 2 * g + i
                    P.dma("sync", KTb[i][:, :], kT[cc * 128:(cc + 1) * 128, s, :], writes=[KTb[i]], semfrom=KTb[i])
                    if s > 0:
                        P.dma("sync", KTb[i][:, 0:1024], kT[cc * 128:(cc + 1) * 128, s - 1, 2048:3072], writes=[KTb[i]], semfrom=KTb[i])
                    if s < NSEG - 1:
                        P.dma("sync", KTb[i][:, 3072:4096], kT[cc * 128:(cc + 1) * 128, s + 1, 1024:2048], writes=[KTb[i]], semfrom=KTb[i])
                    P.dma("sync", QTb[i][:, :], qT[cc * 128:(cc + 1) * 128, s * SEG:(s + 1) * SEG], writes=[QTb[i]], semfrom=QTb[i])
                pend = []
                for r in range(d):
                    for c in range(nchunk):
                        vt = vts.next()
                        P.dma("sync", vt[:, :], vG[g][s, r, 128 * c:128 * c + 128, :], writes=[vt], semfrom=vt)
                        val = ones_bf
                        if c == 0:
                            if s > 0:
                                P.dma("sync", vt[0:64, :], vG[g][s - 1, r, n:n + 64, :], writes=[vt], semfrom=vt)
                            P.I("gpsimd", "tensor_scalar",
                                vt[:, :], vt[:, :], flags[:, 4 * s + 2:4 * s + 3], None, ALU.mult,
                                reads=[vt, flags], writes=[vt])
                            val = valL
                        if c == nchunk - 1:
                            if s < NSEG - 1:
                                P.dma("sync", vt[64:128, :], vG[g][s + 1, r, 64:128, :], writes=[vt], semfrom=vt)
                            P.I("gpsimd", "tensor_scalar",
                                vt[:, :], vt[:, :], flags[:, 4 * s + 3:4 * s + 4], None, ALU.mult,
                                reads=[vt, flags], writes=[vt])
                            val = valR
                        qs0 = max(128 * c - 128, 0)
                        qs1 = min(128 * c + 128, n)
                        NQ = qs1 - qs0
                        qoff = qs0 - (128 * c - 128)
                        kcol = 1024 + (128 * c - 64) * d + r
                        qcol = qs0 * d + r
                        for hs in range(4):
                            i = hs // 2
                            p0 = 64 * (hs % 2)
                            ps = prot.next()
                            lhs = KTb[i][p0:p0 + 64, kcol:kcol + 127 * d + 1:d]
                            rhs = QTb[i][p0:p0 + 64, qcol:qcol + (NQ - 1) * d + 1:d]
                            P.I("tensor", "matmul",
                                ps[:, 0:NQ], lhs, rhs, start=True, stop=False, reads=[KTb[i], QTb[i]], writes=[ps])
                            P.I("tensor", "matmul",
                                ps[:, 0:NQ], ident[:, :], bias8[:, 4 * g + hs, qoff:qoff + NQ], start=False, stop=True,
                                reads=[ident, bias8], writes=[ps])
                            E = Es.next()
                            P.I("scalar", "activation", E[:, 0:NQ], ps[:, 0:NQ], AF.Exp, scale=0.125,
                                 reads=[ps], writes=[E])

                            def stage2(vt=vt, val=val, E=E, NQ=NQ, hs=hs, qcol=qcol, d=d):
                                pn = prot.next()
                                P.I("tensor", "matmul",
                                    pn[0:64, 0:NQ], vt[:, hs * 64:(hs + 1) * 64], E[:, 0:NQ], start=True, stop=True,
                                    reads=[vt, E], writes=[pn])
                                P.I("tensor", "matmul",
                                    pn[0:64, 256:256 + NQ], val[:, 0:64], E[:, 0:NQ], start=True, stop=True,
                                    reads=[val, E], writes=[pn])
                                acc = ndacc[hs][:, :, qcol:qcol + (NQ - 1) * d + 1:d]
                                src_ = pn[0:64, :].rearrange("p (t q) -> p t q", t=2)[:, :, 0:NQ]
                                P.I("vector", "tensor_tensor", acc, src_, acc, ALU.add,
                                     reads=[pn, ndacc[hs]], writes=[ndacc[hs]])
                            pend.append(stage2)
                            if len(pend) > 2:
                                pend.pop(0)()
                            unit_ctr[0] += 1
                            if PREPQ and unit_ctr[0] % 12 == 0:
                                PREPQ.pop(0)()
                while pend:
                    pend.pop(0)()
            for hs in range(4):
                P.I("vector", "reciprocal", ndacc[hs][:, 1, :], ndacc[hs][:, 1, :], reads=[ndacc[hs]], writes=[ndacc[hs]])
                o = ost.next()
                P.I("vector", "tensor_tensor", o[:, :], ndacc[hs][:, 0, :], ndacc[hs][:, 1, :], ALU.mult,
                     reads=[ndacc[hs]], writes=[o])
                P.dma("sync", attnT[hs * 64:(hs + 1) * 64, s * SEG:(s + 1) * SEG], o[:, :], reads=[o], semfrom=o)

        while PREPQ:
            PREPQ.pop(0)()
        sbp.close()
        sbp = P.scope()
        mem32 = sbp.sbuf("b_mem32", [128, 2, D], F32)
        WSH[0] = WStream(sbp, 3, 1, "B")
        for s in range(NSEG):
            WSH[0].plan([gid(layer, ("mkv", 0)), gid(layer, ("mkv", 1))])
        memb = sbp.sbuf("b_memb", [128, 2, D], BF16)
        memT = sbp.sbuf("b_memT", [128, 8, 256], BF16)
        KmT = sbp.sbuf("b_KmT", [128, 4, 256], BF16)
        Vm = sbp.sbuf("b_Vm", [128, 2, 512], BF16)
        qmt = Rot([sbp.sbuf("b_qm%d" % i, [128, 4, TT], BF16) for i in range(2)])
        Em = Rot([sbp.sbuf("b_Em%d" % i, [128, TT], BF16) for i in range(4)])
        rden = Rot([sbp.sbuf("b_rden%d" % i, [128, TT], F32) for i in range(2)])
        xmst = Rot([sbp.sbuf("b_xmst%d" % i, [128, TT], BF16) for i in range(2)])
        for s in range(NSEG):
            P.dma("sync", mem32[:, :, :], mem_in[s].rearrange("(b p) d -> p b d", p=128), writes=[mem32], semfrom=mem32)
            for b in range(2):
                evac(b, memb[:, b, :], mem32[:, b, :], [mem32], [memb])
            for kc in range(8):
                pb = prot.next()
                pv = pb[:, :].bitcast(BF16)
                for b in range(2):
                    P.I("tensor", "transpose",
                        pv[:, b * 128:(b + 1) * 128], memb[:, b, kc * 128:(kc + 1) * 128], ident[:, :],
                        reads=[memb, ident], writes=[pb])
                evac(kc, memT[:, kc, :], pv[:, 0:256], [pb], [memT])
            slot = WSH[0].get(gid(layer, ("mkv", 0)))
            wv = slot[:, :].rearrange("p (kc n) -> p kc n", kc=8)
            for h in range(4):
                pb = prot.next()
                for kc in range(8):
                    P.I("tensor", "matmul",
                        pb[:, 0:256], wv[:, kc, h * 128:(h + 1) * 128], memT[:, kc, :], start=(kc == 0), stop=(kc == 7),
                        reads=[slot, memT], writes=[pb])
                evac(h, KmT[:, h, :], pb[:, 0:256], [pb], [KmT])
            slot = WSH[0].get(gid(layer, ("mkv", 1)))
            wv = slot[:, :].rearrange("p (kc n) -> p kc n", kc=8)
            for mc in range(2):
                pb = prot.next()
                for kc in range(8):
                    P.I("tensor", "matmul",
                        pb[:, :], memT[:, kc, mc * 128:(mc + 1) * 128], wv[:, kc, :], start=(kc == 0), stop=(kc == 7),
                        reads=[slot, memT], writes=[pb])
                evac(mc, Vm[:, mc, :], pb[:, :], [pb], [Vm])
            for tt in range(TPS):
                t0 = s * SEG + tt * TT
                qm = qmt.next()
                P.dma("sync", qm[:, :, :], qmT[:, t0:t0 + TT].rearrange("(h p) t -> p h t", p=128), writes=[qm], semfrom=qm)
                for h in range(4):
                    ems = []
                    for mc in range(2):
                        pb = prot.next()
                        P.I("tensor", "matmul",
                            pb[:, :], KmT[:, h, mc * 128:(mc + 1) * 128], qm[:, h, :], start=True, stop=True,
                            reads=[KmT, qm], writes=[pb])
                        em = Em.next()
                        P.I("scalar", "activation", em[:, :], pb[:, :], AF.Exp, scale=128.0 ** -0.5,
                             reads=[pb], writes=[em])
                        ems.append(em)
                    pn = prot.next()
                    pd = prot.next()
                    for mc in range(2):
                        P.I("tensor", "matmul",
                            pn[:, :], Vm[:, mc, h * 128:(h + 1) * 128], ems[mc][:, :], start=(mc == 0), stop=(mc == 1),
                            reads=[Vm, ems[mc]], writes=[pn])
                    for mc in range(2):
                        P.I("tensor", "matmul",
                            pd[:, :], ones_bf[:, :], ems[mc][:, :], start=(mc == 0), stop=(mc == 1),
                            reads=[ones_bf, ems[mc]], writes=[pd])
                    rd = rden.next()
                    P.I("vector", "reciprocal", rd[:, :], pd[:, :], reads=[pd], writes=[rd])
                    xo = xmst.next()
                    P.I("vector", "tensor_tensor", xo[:, :], pn[:, :], rd[:, :], ALU.mult,
                         reads=[pn, rd], writes=[xo])
                    P.dma("sync", xmT[h * 128:(h + 1) * 128, t0:t0 + TT], xo[:, :], reads=[xo], semfrom=xo)
        sbp.close()

        sl = P.scope()
        Wbd = sl.sbuf("l_W", [128, 24, 128], BF16)
        P.dma("sync", Wbd[:, :, :], wb[gid(layer, ("lru", 0)), :, 0:3072].rearrange("p (c j) -> p c j", c=24),
              reads=[wbt[gid(layer, ("lru", 0))]], writes=[Wbd], semfrom=Wbd)
        cw = sl.sbuf("l_cw", [128, 24], F32)
        cb = sl.sbuf("l_cb", [128, 6], F32)
        hba = sl.sbuf("l_hba", [128, 12], F32)
        hbx = sl.sbuf("l_hbx", [128, 12], F32)
        lam = sl.sbuf("l_lam", [128, 12], F32)
        hsc = sl.sbuf("l_hsc", [128, 12], F32)
        hsc2 = sl.sbuf("l_hsc2", [128, 12], F32)
        P.dma("sync", cw[:, :], convw_c[layer], writes=[cw], semfrom=cw)
        P.dma("sync", cb[:, :], convb_c[layer], writes=[cb], semfrom=cb)
        P.dma("sync", hba[:, :], lba_c[layer], writes=[hba], semfrom=hba)
        P.dma("sync", hbx[:, :], lbx_c[layer], writes=[hbx], semfrom=hbx)
        P.dma("sync", lam[:, :], llam_c[layer], writes=[lam], semfrom=lam)
        P.I("vector", "tensor_scalar", hba[:, :], hba[:, :], 0.5, None, ALU.mult, reads=[hba], writes=[hba])
        P.I("vector", "tensor_scalar", hbx[:, :], hbx[:, :], 0.5, None, ALU.mult, reads=[hbx], writes=[hbx])
        P.I("scalar", "activation", lam[:, :], lam[:, :], AF.Exp, scale=-1.0, reads=[lam], writes=[lam])
        P.I("scalar", "activation", lam[:, :], lam[:, :], AF.Ln, bias=1.0, reads=[lam], writes=[lam])
        P.I("vector", "tensor_scalar", hsc[:, :], lam[:, :], -4.0, None, ALU.mult, reads=[lam], writes=[hsc])
        P.I("vector", "tensor_scalar", hsc2[:, :], lam[:, :], -8.0, None, ALU.mult, reads=[lam], writes=[hsc2])
        Dg = sl.sbuf("l_Dg", [128, 24, 128], BF16)
        for idx in range(24):
            P.I("vector", "tensor_scalar", Dg[:, idx, :], ident[:, :], cw[:, idx:idx + 1], None, ALU.mult,
                reads=[ident, cw], writes=[Dg])
        carry = [sl.sbuf("l_carry%d" % i, [128, 6], F32) for i in range(2)]
        for i in range(2):
            P.I("vector", "memset", carry[i][:, :], 0.0, writes=[carry[i]])
        xrs = Rot([sl.sbuf("l_xr%d" % i, [128, TT + 4], BF16) for i in range(8)])
        NB = 12
        xcs = Rot([sl.sbuf("l_xc%d" % i, [128, TT], F32) for i in range(7)])
        xcbs = Rot([sl.sbuf("l_xcb%d" % i, [128, TT], BF16) for i in range(7)])
        tra = Rot([sl.sbuf("l_tra%d" % i, [128, TT], F32) for i in range(3)])
        tri = Rot([sl.sbuf("l_tri%d" % i, [128, TT], F32) for i in range(NB)])
        aa = Rot([sl.sbuf("l_a%d" % i, [128, TT], F32) for i in range(NB)])
        a2 = Rot([sl.sbuf("l_a2%d" % i, [128, TT], F32) for i in range(NB)])
        uu = Rot([sl.sbuf("l_u%d" % i, [128, TT], F32) for i in range(3)])
        hh_ = Rot([sl.sbuf("l_h%d" % i, [128, TT], F32) for i in range(3)])
        hfl = Rot([sl.sbuf("l_hf%d" % i, [128, TT], F32) for i in range(3)])
        hso = Rot([sl.sbuf("l_hs%d" % i, [128, TT], BF16) for i in range(3)])

        def lru_stage1a(di, ti):
            s = ti // TPS
            tl = (ti % TPS) * TT
            keep = []
            t0 = ti * TT
            if di == 1:
                for ch in range(6):
                    xc = xcs.next()
                    P.dma("sync", xc[:, :], xcT[ch * 128:(ch + 1) * 128, t0:t0 + TT], writes=[xc], semfrom=xc)
                    xcb = xcbs.next()
                    P.I("vector", "tensor_copy", xcb[:, :], xc[:, :], reads=[xc], writes=[xcb])
                    keep.append([xc, xcb])
                return keep
            xrl = []
            for ch in range(6):
                xr = xrs.next()
                P.dma("sync", xr[:, :], xrT[ch * 128:(ch + 1) * 128, s, tl:tl + TT + 4], writes=[xr], semfrom=xr)
                xrl.append(xr)
            for ch in range(6):
                xr = xrl[ch]
                if tl == 0:
                    if s > 0:
                        P.dma("sync", xr[:, 0:1], xrT[ch * 128:(ch + 1) * 128, s - 1, SEG:SEG + 1], writes=[xr], semfrom=xr)
                    P.I("gpsimd", "tensor_scalar", xr[:, 0:1], xr[:, 0:1], flags[:, 4 * s:4 * s + 1], None, ALU.mult,
                        reads=[xr, flags], writes=[xr])
                if tl == SEG - TT:
                    if s < NSEG - 1:
                        P.dma("sync", xr[:, TT + 1:TT + 3], xrT[ch * 128:(ch + 1) * 128, s + 1, 1:3], writes=[xr], semfrom=xr)
                    P.I("gpsimd", "tensor_scalar", xr[:, TT + 1:TT + 3], xr[:, TT + 1:TT + 3], flags[:, 4 * s + 1:4 * s + 2], None, ALU.mult,
                        reads=[xr, flags], writes=[xr])
            for ch in range(6):
                xr = xrl[ch]
                if not PE_CONV:
                    xc = xcs.next()
                    P.I("vector", "tensor_scalar",
                        xc[:, :], xr[:, 0:TT], cw[:, ch * 4:ch * 4 + 1], cb[:, ch:ch + 1], ALU.mult, ALU.add,
                        reads=[xr, cw, cb], writes=[xc])
                    for j in range(1, 4):
                        P.I("vector", "scalar_tensor_tensor",
                            xc[:, :], xr[:, j:j + TT], cw[:, ch * 4 + j:ch * 4 + j + 1], xc[:, :], ALU.mult, ALU.add,
                            reads=[xr, cw, xc], writes=[xc])
                    xcb = xcbs.next()
                    P.I("scalar", "copy", xcb[:, :], xc[:, :], reads=[xc], writes=[xcb])
                    P.dma("sync", xcT[ch * 128:(ch + 1) * 128, t0:t0 + TT], xc[:, :], reads=[xc], semfrom=xc)
                    keep.append([xc, xcb])
                    continue
                pc_ = prot.next()
                for j in range(4):
                    P.I("tensor", "matmul", pc_[:, :], Dg[:, ch * 4 + j, :], xr[:, j:j + TT], start=(j == 0), stop=(j == 3),
                        reads=[Dg, xr], writes=[pc_])
                xc = xcs.next()
                P.I("scalar", "activation", xc[:, :], pc_[:, :], AF.Identity, bias=cb[:, ch:ch + 1],
                    reads=[pc_, cb], writes=[xc])
                xcb = xcbs.next()
                P.I("vector", "tensor_scalar", xcb[:, :], pc_[:, :], cb[:, ch:ch + 1], None, ALU.add,
                    reads=[pc_, cb], writes=[xcb])
                P.dma("sync", xcT[ch * 128:(ch + 1) * 128, t0:t0 + TT], xc[:, :], reads=[xc], semfrom=xc)
                keep.append([xc, xcb])
            return keep

        def lru_stage1b(di, ti, keep):
            for ch in range(6):
                xc, xcb = keep[ch]
                pa = prot.next()
                px = prot.next()
                P.I("tensor", "matmul",
                    pa[:, :], Wbd[:, (di * 2 + 0) * 6 + ch, :], xcb[:, :], start=True, stop=True,
                    reads=[Wbd, xcb], writes=[pa])
                P.I("tensor", "matmul",
                    px[:, :], Wbd[:, (di * 2 + 1) * 6 + ch, :], xcb[:, :], start=True, stop=True,
                    reads=[Wbd, xcb], writes=[px])
                ta = tra.next()
                tx = tri.next()
                col = di * 6 + ch
                P.I("scalar", "activation", ta[:, :], pa[:, :], AF.Tanh, bias=hba[:, col:col + 1], scale=0.5,
                    reads=[pa, hba], writes=[ta])
                P.I("scalar", "activation", tx[:, :], px[:, :], AF.Tanh, bias=hbx[:, col:col + 1], scale=0.5,
                    reads=[px, hbx], writes=[tx])
                a = aa.next()
                asq = a2.next()
                P.I("scalar", "activation", a[:, :], ta[:, :], AF.Exp, bias=hsc[:, col:col + 1], scale=hsc[:, col:col + 1],
                    reads=[ta, hsc], writes=[a])
                P.I("scalar", "activation", asq[:, :], ta[:, :], AF.Exp, bias=hsc2[:, col:col + 1], scale=hsc2[:, col:col + 1],
                    reads=[ta, hsc2], writes=[asq])
                P.I("scalar", "activation", asq[:, :], asq[:, :], AF.Relu, bias=1.0, scale=-1.0,
                    reads=[asq], writes=[asq])
                P.I("gpsimd", "tensor_tensor", tx[:, :], tx[:, :], xc[:, :], ALU.mult, reads=[tx, xc], writes=[tx])
                P.I("gpsimd", "tensor_tensor", tx[:, :], tx[:, :], xc[:, :], ALU.add, reads=[tx, xc], writes=[tx])
                keep[ch] += [tx, a, asq]

        def lru_stage2(di, ti, keep):
            s = ti // TPS
            t0 = ti * TT
            first_of_seg = (ti % TPS == 0) if di == 0 else (ti % TPS == TPS - 1)
            if first_of_seg:
                fcol = 4 * s + (0 if di == 0 else 1)
                P.I("vector", "tensor_scalar",
                    carry[di][:, :], carry[di][:, :], flags[:, fcol:fcol + 1], None, ALU.mult,
                    reads=[carry[di], flags], writes=[carry[di]])
            for ch in range(6):
                xc, xcb, tx, a, asq = keep[ch]
                P.I("scalar", "activation", asq[:, :], asq[:, :], AF.Sqrt, reads=[asq], writes=[asq])
                u = uu.next()
                P.I("vector", "scalar_tensor_tensor", u[:, :], tx[:, :], 0.5, asq[:, :], ALU.mult, ALU.mult,
                    reads=[tx, asq], writes=[u])
                h = hh_.next()
                if di == 0:
                    P.I("vector", "tensor_tensor_scan",
                        h[:, :], a[:, :], u[:, :], carry[0][:, ch:ch + 1], ALU.mult, ALU.add,
                        reads=[a, u, carry[0]], writes=[h])
                    P.I("vector", "tensor_copy", carry[0][:, ch:ch + 1], h[:, TT - 1:TT],
                        reads=[h], writes=[carry[0]])
                    P.dma("sync", hfT[ch * 128:(ch + 1) * 128, t0:t0 + TT], h[:, :], reads=[h], semfrom=h)
                else:
                    P.I("vector", "tensor_tensor_scan",
                        h[:, ::-1], a[:, ::-1], u[:, ::-1], carry[1][:, ch:ch + 1], ALU.mult, ALU.add,
                        reads=[a, u, carry[1]], writes=[h])
                    P.I("vector", "tensor_copy", carry[1][:, ch:ch + 1], h[:, 0:1],
                        reads=[h], writes=[carry[1]])
                    hf = hfl.next()
                    P.dma("sync", hf[:, :], hfT[ch * 128:(ch + 1) * 128, t0:t0 + TT], writes=[hf], semfrom=hf)
                    ho = hso.next()
                    P.I("vector", "tensor_tensor", ho[:, :], hf[:, :], h[:, :], ALU.add,
                        reads=[hf, h], writes=[ho])
                    P.dma("sync", hsT[ch * 128:(ch + 1) * 128, t0:t0 + TT], ho[:, :], reads=[ho], semfrom=ho)

        for di in range(2):
            order = list(range(NT)) if di == 0 else list(range(NT - 1, -1, -1))
            prev = None
            for ti in order:
                keep = lru_stage1a(di, ti)
                if prev is not None:
                    lru_stage2(di, prev[0], prev[1])
                lru_stage1b(di, ti, keep)
                prev = (ti, keep)
            lru_stage2(di, prev[0], prev[1])
            if di == 0:
                P.barrier()
        sl.close()

        scs = P.scope()
        lng, lnb = load_ln(scs, [1, 2])
        WSH[0] = WStream(scs, 7, 4, "C")
        for ti in range(NT):
            seq = [("g", 0), ("g", 1), ("bra", 0), ("g", 2), ("g", 3), ("brl", 0), ("brl", 1),
                   ("g", 4), ("g", 5), ("brm", 0), ("wo", 0), ("wo", 1)]
            seq += [("ffi", 1, j) for j in range(11)] + [("ffo", 1, j) for j in range(6)]
            WSH[0].plan([gid(layer, n) for n in seq])

        xres = scs.sbuf("c_x", [128, 4, D], F32)
        xb = scs.sbuf("c_xb", [128, 4, D], BF16)
        hT = scs.sbuf("c_hT", [128, NFF, TT], BF16)
        x1t = scs.sbuf("c_x1T", [128, 8, TT], BF16)
        xT = x1t
        at2 = [scs.sbuf("c_at%d" % i, [64, 4, TT], BF16) for i in range(2)]
        hs2 = [scs.sbuf("c_hs%d" % i, [128, 6, TT], BF16) for i in range(2)]
        gr2 = [scs.sbuf("c_gr%d" % i, [128, 6, TT], BF16) for i in range(2)]
        xm2 = [scs.sbuf("c_xm%d" % i, [128, 4, TT], BF16) for i in range(2)]
        macc = scs.sbuf("c_macc", [128, 8, TT], F32)
        mT = scs.sbuf("c_mT", [128, 8, TT], BF16)
        bgt = scs.sbuf("c_bg", [128, 24], F32)
        P.dma("sync", bgt[:, :], bgate_c[layer], writes=[bgt], semfrom=bgt)
        P.I("vector", "tensor_scalar", bgt[:, :], bgt[:, :], 0.5, None, ALU.mult, reads=[bgt], writes=[bgt])
        g1 = Rot([scs.sbuf("c_g1_%d" % i, [128, TT], F32) for i in range(2)])
        g2 = Rot([scs.sbuf("c_g2_%d" % i, [128, TT], F32) for i in range(2)])
        scb = {
            "stats": scs.sbuf("c_stats", [128, 4, 2, 6], F32),
            "mv": scs.sbuf("c_mv", [128, 4, 2], F32),
            "rstd": scs.sbuf("c_rstd", [128, 4], F32),
            "nmr": scs.sbuf("c_nmr", [128, 4], F32),
            "mhalf": mhalf,
            "tth": g1,
            "tv": g2,
        }
        def c_load_x(ti):
            t0 = ti * TT
            P.dma("sync", xres[:, :, :], x1s[t0:t0 + TT, :].rearrange("(b p) d -> p b d", p=128), writes=[xres], semfrom=xres)

        def c_load_x1t(ti):
            t0 = ti * TT
            P.dma("sync", x1t[:, :, :], x1T[:, t0:t0 + TT].rearrange("(kc p) t -> p kc t", p=128), writes=[x1t], semfrom=x1t)

        def c_pre(ti):
            t0 = ti * TT
            at_, hst, grt, xmt = at2[ti % 2], hs2[ti % 2], gr2[ti % 2], xm2[ti % 2]
            P.dma("sync", at_[:, :, :], attnT[:, t0:t0 + TT].rearrange("(h p) t -> p h t", p=64), writes=[at_], semfrom=at_)
            P.dma("sync", hst[:, :, :], hsT[:, t0:t0 + TT].rearrange("(c p) t -> p c t", p=128), writes=[hst], semfrom=hst)
            P.dma("sync", grt[:, :, :], grT[:, t0:t0 + TT].rearrange("(c p) t -> p c t", p=128), writes=[grt], semfrom=grt)
            P.dma("sync", xmt[:, :, :], xmT[:, t0:t0 + TT].rearrange("(c p) t -> p c t", p=128), writes=[xmt], semfrom=xmt)
            for ch in range(6):
                DEFER.append(lambda ch=ch, grt=grt, hst=hst: c_pre_chunk(ch, grt, hst))

        def c_pre_chunk(ch, grt, hst):
            if True:
                t1 = g1.next()
                t2 = g2.next()
                P.I("gpsimd", "tensor_tensor", t1[:, :], grt[:, ch, :], grt[:, ch, :], ALU.mult,
                     reads=[grt], writes=[t1])
                P.I("gpsimd", "tensor_scalar", t1[:, :], t1[:, :], 0.044715, 1.0, ALU.mult, ALU.add,
                     reads=[t1], writes=[t1])
                P.I("gpsimd", "tensor_tensor", t1[:, :], t1[:, :], grt[:, ch, :], ALU.mult,
                     reads=[t1, grt], writes=[t1])
                P.I("scalar", "activation", t2[:, :], t1[:, :], AF.Tanh, scale=GELU_C,
                     reads=[t1], writes=[t2])
                P.I("vector", "scalar_tensor_tensor", t2[:, :], t2[:, :], 1.0, grt[:, ch, :], ALU.add, ALU.mult,
                     reads=[t2, grt], writes=[t2])
                P.I("vector", "scalar_tensor_tensor", grt[:, ch, :], t2[:, :], 0.5, hst[:, ch, :], ALU.mult, ALU.mult,
                     reads=[t2, hst], writes=[grt])

        c_load_x(0)
        c_load_x1t(0)
        c_pre(0)
        flush()
        for ti in range(NT):
            t0 = ti * TT
            at_, hst, grt, xmt = at2[ti % 2], hs2[ti % 2], gr2[ti % 2], xm2[ti % 2]
            gslots = {}
            for br in range(3):
                gsl = [WSH[0].get(gid(layer, ("g", 2 * br))), WSH[0].get(gid(layer, ("g", 2 * br + 1)))]
                if br == 0:
                    bsl = [WSH[0].get(gid(layer, ("bra", 0)))]
                elif br == 1:
                    bsl = [WSH[0].get(gid(layer, ("brl", 0))), WSH[0].get(gid(layer, ("brl", 1)))]
                else:
                    bsl = [WSH[0].get(gid(layer, ("brm", 0)))]
                for dc in range(8):
                    pg = prot.next()
                    gs = gsl[dc // 4]
                    gv = gs[:, :].rearrange("p (kc n) -> p kc n", kc=8)
                    for kc in range(8):
                        P.I("tensor", "matmul",
                            pg[:, :], gv[:, kc, (dc % 4) * 128:(dc % 4 + 1) * 128], x1t[:, kc, :], start=(kc == 0), stop=(kc == 7),
                            reads=[gs, x1t], writes=[pg])
                    pp = prot.next()
                    if br == 0:
                        bv = bsl[0][0:64, :].rearrange("p (h n) -> p h n", h=4)
                        for h in range(4):
                            P.I("tensor", "matmul",
                                pp[:, :], bv[:, h, dc * 128:(dc + 1) * 128], at_[:, h, :], start=(h == 0), stop=(h == 3),
                                reads=[bsl[0], at_], writes=[pp])
                    elif br == 1:
                        for kc in range(6):
                            sl_ = bsl[kc // 4]
                            nk = 4 if kc < 4 else 2
                            bv = sl_[:, 0:nk * 1024].rearrange("p (kc n) -> p kc n", kc=nk)
                            P.I("tensor", "matmul",
                                pp[:, :], bv[:, kc % 4, dc * 128:(dc + 1) * 128], grt[:, kc, :], start=(kc == 0), stop=(kc == 5),
                                reads=[sl_, grt], writes=[pp])
                    else:
                        bv = bsl[0][:, :].rearrange("p (kc n) -> p kc n", kc=4)
                        for kc in range(4):
                            P.I("tensor", "matmul",
                                pp[:, :], bv[:, kc, dc * 128:(dc + 1) * 128], xmt[:, kc, :], start=(kc == 0), stop=(kc == 3),
                                reads=[bsl[0], xmt], writes=[pp])
                    drain(1)
                    th = g1.next()
                    col = br * 8 + dc
                    P.I("scalar", "activation", th[:, :], pg[:, :], AF.Tanh, bias=bgt[:, col:col + 1], scale=0.5,
                         reads=[pg, bgt], writes=[th])
                    if br == 0:
                        P.I("vector", "scalar_tensor_tensor",
                            macc[:, dc, :], th[:, :], 1.0, pp[:, :], ALU.add, ALU.mult, reads=[th, pp], writes=[macc])
                    else:
                        tv_ = g2.next()
                        P.I("vector", "scalar_tensor_tensor",
                            tv_[:, :], th[:, :], 1.0, pp[:, :], ALU.add, ALU.mult, reads=[th, pp], writes=[tv_])
                        if br == 1:
                            P.I("gpsimd", "tensor_tensor", macc[:, dc, :], macc[:, dc, :], tv_[:, :], ALU.add,
                                 reads=[macc, tv_], writes=[macc])
                        else:
                            P.I("gpsimd", "tensor_tensor", mT[:, dc, :], macc[:, dc, :], tv_[:, :], ALU.add,
                                 reads=[macc, tv_], writes=[mT])
            flush()
            wo = [WSH[0].get(gid(layer, ("wo", 0))), WSH[0].get(gid(layer, ("wo", 1)))]
            for b in range(4):
                for hh in range(2):
                    pb = prot.next()
                    for kc in range(8):
                        sl_ = wo[kc // 4]
                        wv = sl_[:, :].rearrange("p (kc n) -> p kc n", kc=4)
                        P.I("tensor", "matmul",
                            pb[:, :], mT[:, kc, b * 128:(b + 1) * 128], wv[:, kc % 4, hh * 512:(hh + 1) * 512],
                            start=(kc == 0), stop=(kc == 7), reads=[sl_, mT], writes=[pb])
                    P.I("vector", "scalar_tensor_tensor",
                        xres[:, b, hh * 512:(hh + 1) * 512], pb[:, :], C0, xres[:, b, hh * 512:(hh + 1) * 512],
                        ALU.mult, ALU.add, reads=[pb, xres], writes=[xres])
            layer_norm(scb, xres, layer, 1, lng[1], lnb[1], xb)
            transposes(scs, xb, xT)
            if ti + 1 < NT:
                c_pre(ti + 1)
            ffn_in(scb, layer, 1, xT, hT)
            flush()
            if ti + 1 < NT:
                c_load_x1t(ti + 1)
            ffn_out(scb, layer, 1, xres, hT)
            DEFER.extend(ln_steps(scb, xres, lng[2], lnb[2], None))
            DEFER.append(lambda t0=t0: P.dma("sync", x_dst[t0:t0 + TT, :].rearrange("(b p) d -> p b d", p=128),
                                             xres[:, :, :], reads=[xres], semfrom=xres))
            if ti + 1 < NT:
                DEFER.append(lambda ti=ti: c_load_x(ti + 1))
            if not DEFER_LN:
                flush()
        flush()
        scs.close()
        scl.close()

    P.barrier()
    P.emit()
    return nc, P


def _prep_inputs(inp, NSEG, core_seg_specs):
    f32 = np.float32
    rel_bias = np.asarray(inp["rel_bias"], f32)

    def t5_bucket(rel):
        half = 16
        max_exact = 8
        sign = (rel > 0).astype(np.int32) * half
        n = np.abs(rel)
        large = max_exact + (np.log(np.maximum(n, 1) / max_exact) / math.log(1024 / max_exact)
                             * (half - max_exact)).astype(np.int32)
        large = np.minimum(large, half - 1)
        return sign + np.where(n < max_exact, n, large)

    kk = np.arange(128)[:, None]
    qq = np.arange(256)[None, :]
    delta = kk - qq + 64
    biasg = np.zeros((12, 128, 256), f32)
    for g, d in enumerate(DILS):
        bk = t5_bucket(delta * d)
        for hs in range(4):
            biasg[g * 4 + hs] = rel_bias[bk, g * 4 + hs]
    maskc = np.where(np.abs(delta) <= 64, 0.0, -1e30).astype(f32)
    ident = np.eye(128, dtype=np.float32).astype(ml_dtypes.bfloat16)

    def colmajor(v, nchunk):
        return v

    b_gate = np.asarray(inp["b_gate"], f32)
    bgate_c = b_gate.reshape(2, 3, 8, 128).transpose(0, 3, 1, 2).reshape(2, 128, 24)
    conv_w = np.asarray(inp["conv_w"], f32)
    convw_c = conv_w.reshape(2, 4, 6, 128).transpose(0, 3, 2, 1).reshape(2, 128, 24)
    conv_b = np.asarray(inp["conv_b"], f32)
    convb_c = conv_b.reshape(2, 6, 128).transpose(0, 2, 1)

    def dirvec(v):
        return np.asarray(v, f32).reshape(2, 2, 6, 128).transpose(0, 3, 1, 2).reshape(2, 128, 12)

    lng_r = np.broadcast_to(np.asarray(inp["ln_g"], f32)[:, :, None, :], (2, 3, 128, D))
    lnb_r = np.broadcast_to(np.asarray(inp["ln_b"], f32)[:, :, None, :], (2, 3, 128, D))
    shared = {
        "w_in": np.asarray(inp["w_in"], f32), "ff_in": np.asarray(inp["ff_in"], f32),
        "ff_out": np.asarray(inp["ff_out"], f32), "w_mem_kv": np.asarray(inp["w_mem_kv"], f32),
        "w_br_attn": np.asarray(inp["w_br_attn"], f32), "w_br_lru": np.asarray(inp["w_br_lru"], f32),
        "w_br_mem": np.asarray(inp["w_br_mem"], f32), "w_out": np.asarray(inp["w_out"], f32),
        "lru_wa": np.asarray(inp["lru_wa"], f32), "lru_wx": np.asarray(inp["lru_wx"], f32),
        "bgate_c": np.ascontiguousarray(bgate_c), "convw_c": np.ascontiguousarray(convw_c),
        "convb_c": np.ascontiguousarray(convb_c), "lba_c": np.ascontiguousarray(dirvec(inp["lru_ba"])),
        "lbx_c": np.ascontiguousarray(dirvec(inp["lru_bx"])), "llam_c": np.ascontiguousarray(dirvec(inp["lru_lambda"])),
        "lng_r": np.ascontiguousarray(lng_r), "lnb_r": np.ascontiguousarray(lnb_r),
        "biasg": biasg, "maskc": maskc, "ident": ident,
    }
    in_maps = []
    for specs in core_seg_specs:
        xs = np.concatenate([sp[0] for sp in specs], axis=0)
        mems = np.stack([sp[1] for sp in specs], axis=0)
        fl = np.zeros((128, NSEG * 4), f32)
        for s, sp in enumerate(specs):
            fl[:, 4 * s + 0] = sp[2]
            fl[:, 4 * s + 1] = sp[3]
            fl[:64, 4 * s + 2] = sp[2]
            fl[64:, 4 * s + 2] = 1.0
            fl[:64, 4 * s + 3] = 1.0
            fl[64:, 4 * s + 3] = sp[3]
        m = dict(shared)
        m["x_in"] = np.ascontiguousarray(xs, dtype=f32)
        m["mem_in"] = np.ascontiguousarray(mems, dtype=f32)
        m["flags"] = fl
        in_maps.append(m)
    return in_maps


_CACHE = {}


def kernel(**inp):
    NSEG = 4
    xp = np.asarray(inp["x_prompt"], np.float32)
    xs = np.asarray(inp["x_sample"], np.float32)
    mp = np.asarray(inp["mem_prompt"], np.float32)
    ms = np.asarray(inp["mem_sample"], np.float32)
    specs = []
    for b in range(2):
        specs.append([(xp[b, q * SEG:(q + 1) * SEG], mp[b], 1.0 if q > 0 else 0.0, 1.0 if q < 3 else 0.0)
                      for q in range(4)])
    for c in range(2):
        specs.append([(xs[4 * c + j], ms[4 * c + j], 0.0, 0.0) for j in range(4)])
    for c in range(4):
        specs.append(specs[2 + c % 2])
    in_maps = _prep_inputs(inp, NSEG, specs)
    if "nc" not in _CACHE:
        _CACHE["nc"] = build_program(NSEG)[0]
    nc = _CACHE["nc"]
    res = run_bass_kernel_spmd(nc, in_maps, core_ids=list(range(8)))
    ys = [np.asarray(r["y"], np.float32) for r in res.results]
    y_prompt = np.stack([ys[0], ys[1]], axis=0)
    y_sample = np.concatenate([ys[2].reshape(4, SEG, D), ys[3].reshape(4, SEG, D)], axis=0)
    return (y_prompt, y_sample)
```
